# Optimizing a Trainium2 kernel written in Bass

```python
import jax, jax.numpy as jnp
from jax import lax
import numpy as np

D_MODEL = 1024
BATCH = 16
SEQ = 4096
DEPTH = 1

CHUNK = 128
SG_HEADS = 8
SG_WIDTH = D_MODEL
SG_HEAD_DIM = SG_WIDTH // SG_HEADS
CV_WIDTH = D_MODEL
CV_K = 31
FFN_DIM = 2816
FFN_K = 3
EPS = 1e-6
IN_COLS = 2 * SG_WIDTH + 2 * CV_WIDTH + 2 * D_MODEL

kernel_name = "hybrid_gmlp_conformer_convffn_encoder"


def rms_norm(x, g):
    xf = x.astype(jnp.float32)
    y = xf * lax.rsqrt(jnp.mean(xf * xf, axis=-1, keepdims=True) + EPS)
    return (y * g.astype(jnp.float32)).astype(x.dtype)


def layer_norm(x, g, b):
    xf = x.astype(jnp.float32)
    mu = jnp.mean(xf, axis=-1, keepdims=True)
    var = jnp.mean(jnp.square(xf - mu), axis=-1, keepdims=True)
    y = (xf - mu) * lax.rsqrt(var + EPS)
    return (y * g.astype(jnp.float32) + b.astype(jnp.float32)).astype(x.dtype)


def depthwise_conv(x, w, b):
    k, c = w.shape
    pad = k // 2
    y = lax.conv_general_dilated(
        x, w[:, None, :].astype(x.dtype), window_strides=(1,), padding=[(pad, pad)],
        dimension_numbers=("NWC", "WIO", "NWC"), feature_group_count=c)
    return y + b.astype(x.dtype)


def spatial_gating(u, v, ln_g, ln_b, w_s, b_s):
    bsz, s, _ = v.shape
    v = layer_norm(v, ln_g, ln_b)
    vc = v.reshape(bsz, s // CHUNK, CHUNK, SG_HEADS, SG_HEAD_DIM)
    mixed = jnp.einsum("hij,bnjhd->bnihd", w_s, vc) + b_s.T[:, :, None]
    return u * mixed.reshape(bsz, s, SG_WIDTH)


def conformer_conv(a, g, dw_w, dw_b, ln_g, ln_b):
    h = a * jax.nn.sigmoid(g)
    h = depthwise_conv(h, dw_w, dw_b)
    h = layer_norm(h, ln_g, ln_b)
    return jax.nn.silu(h)


def setup_inputs(seed: int = 0) -> dict:
    key = jax.random.key(seed)
    ks = jax.random.split(key, 20)
    f32 = jnp.float32
    L, D = DEPTH, D_MODEL

    def nrm(k, shape, scale):
        return jax.random.normal(k, shape, f32) * scale

    return {
        "x": jax.random.normal(ks[0], (BATCH, SEQ, D), f32),
        "norm1_g": 1.0 + nrm(ks[1], (L, D), 0.02),
        "w_in": nrm(ks[2], (L, D, IN_COLS), D ** -0.5),
        "sg_ln_g": 1.0 + nrm(ks[3], (L, SG_WIDTH), 0.02),
        "sg_ln_b": nrm(ks[4], (L, SG_WIDTH), 0.02),
        "sg_w": nrm(ks[5], (L, SG_HEADS, CHUNK, CHUNK), CHUNK ** -0.5),
        "sg_b": 1.0 + nrm(ks[6], (L, SG_HEADS, CHUNK), 0.1),
        "w_a_out": nrm(ks[7], (L, SG_WIDTH, D), SG_WIDTH ** -0.5),
        "cv_dw_w": nrm(ks[8], (L, CV_K, CV_WIDTH), CV_K ** -0.5),
        "cv_dw_b": nrm(ks[9], (L, CV_WIDTH), 0.02),
        "cv_ln_g": 1.0 + nrm(ks[10], (L, CV_WIDTH), 0.02),
        "cv_ln_b": nrm(ks[11], (L, CV_WIDTH), 0.02),
        "w_b_out": nrm(ks[12], (L, CV_WIDTH, D), CV_WIDTH ** -0.5),
        "w_o": nrm(ks[13], (L, D, D), D ** -0.5),
        "norm2_g": 1.0 + nrm(ks[14], (L, D), 0.02),
        "w_up": nrm(ks[15], (L, D, 2 * FFN_DIM), D ** -0.5),
        "ffn_dw_w": nrm(ks[16], (L, FFN_K, FFN_DIM), FFN_K ** -0.5),
        "ffn_dw_b": nrm(ks[17], (L, FFN_DIM), 0.02),
        "w_down": nrm(ks[18], (L, FFN_DIM, D), FFN_DIM ** -0.5),
        "final_g": 1.0 + nrm(ks[19], (D,), 0.02),
    }


def reference(x, norm1_g, w_in, sg_ln_g, sg_ln_b, sg_w, sg_b, w_a_out, cv_dw_w, cv_dw_b,
              cv_ln_g, cv_ln_b, w_b_out, w_o, norm2_g, w_up, ffn_dw_w, ffn_dw_b, w_down, final_g):
    split_at = [SG_WIDTH, 2 * SG_WIDTH, 2 * SG_WIDTH + CV_WIDTH,
                2 * SG_WIDTH + 2 * CV_WIDTH, 2 * SG_WIDTH + 2 * CV_WIDTH + D_MODEL]
    h = x
    for l in range(DEPTH):
        n = rms_norm(h, norm1_g[l])
        z = n @ w_in[l]
        z_u, z_v, z_a, z_g, z_ga, z_gb = jnp.split(z, split_at, axis=-1)
        y_a = spatial_gating(jax.nn.gelu(z_u), jax.nn.gelu(z_v), sg_ln_g[l], sg_ln_b[l],
                             sg_w[l], sg_b[l]) @ w_a_out[l]
        y_b = conformer_conv(z_a, z_g, cv_dw_w[l], cv_dw_b[l], cv_ln_g[l], cv_ln_b[l]) @ w_b_out[l]
        merged = jax.nn.sigmoid(z_ga) * y_a + jax.nn.sigmoid(z_gb) * y_b
        h = h + merged @ w_o[l]
        n = rms_norm(h, norm2_g[l])
        gate, val = jnp.split(n @ w_up[l], 2, axis=-1)
        gate = depthwise_conv(gate, ffn_dw_w[l], ffn_dw_b[l])
        h = h + (jax.nn.gelu(gate) * val) @ w_down[l]
    return rms_norm(h, final_g)
```

```python
import numpy as np
from contextlib import ExitStack
import concourse.bass as bass
import concourse.mybir as mybir
from concourse.bass_utils import run_bass_kernel_spmd

F32 = mybir.dt.float32
BF16 = mybir.dt.bfloat16
AF = mybir.ActivationFunctionType
ALU = mybir.AluOpType

D = 1024
SEQ = 4096
NSEQ = 2
T = 512
NT = SEQ // T
FFN = 2816
NFB = FFN // 128
INC = 6144
EPS = 1e-6
HGW = T + 30
N2W = T + 2
HN = N2W // 2


class Sched:
    def __init__(self, nc):
        self.nc = nc
        self.ops = []
        self.eng = {"pe": nc.tensor, "act": nc.scalar, "dve": nc.vector,
                    "pool": nc.gpsimd, "sp": nc.sync}

    def op(self, eng, fn, r=(), w=()):
        self.ops.append((eng, fn, tuple(r), tuple(w)))

    def dma(self, queue, key, fn, r=(), w=()):
        self.ops.append((("dma", queue, key), fn, tuple(r), tuple(w)))

    def dma_keys(self):
        return sorted({e[2] for e, _, _, _ in self.ops if isinstance(e, tuple)})

    def emit(self, sems):
        ops = self.ops
        n = len(ops)
        last_w, readers, last_key = {}, {}, {}
        deps = [None] * n
        needed = [False] * n
        for i, (eng, fn, r, w) in enumerate(ops):
            d = set()
            for x in r:
                if x in last_w:
                    d.add(last_w[x])
            for x in w:
                if x in last_w:
                    d.add(last_w[x])
                d.update(readers.get(x, ()))
            if isinstance(eng, tuple):
                if eng[2] in last_key:
                    d.add(last_key[eng[2]])
                last_key[eng[2]] = i
            d.discard(i)
            if eng == "pe":
                d = {j for j in d if ops[j][0] != "pe"}
            deps[i] = d
            for j in d:
                needed[j] = True
            for x in r:
                readers.setdefault(x, []).append(i)
            for x in w:
                last_w[x] = i
                readers[x] = []
        cnt = {}
        sig = [None] * n
        for i, (eng, fn, r, w) in enumerate(ops):
            if isinstance(eng, tuple):
                key = "dma:" + eng[2]
                cnt[key] = cnt.get(key, 0) + 16
                sig[i] = (key, cnt[key])
            elif needed[i]:
                cnt[eng] = cnt.get(eng, 0) + 1
                sig[i] = (eng, cnt[eng])
        waited = {}
        for i, (eng, fn, r, w) in enumerate(ops):
            issuer = eng[1] if isinstance(eng, tuple) else eng
            e = self.eng[issuer]
            wd = waited.setdefault(issuer, {})
            need = {}
            for j in deps[i]:
                s, v = sig[j]
                if v > need.get(s, 0):
                    need[s] = v
            for s, v in need.items():
                if wd.get(s, 0) < v:
                    e.wait_ge(sems[s], v)
                    wd[s] = v
            inst = fn(e)
            if sig[i] is not None:
                inst.then_inc(sems[sig[i][0]], 16 if isinstance(eng, tuple) else 1)
        return cnt


def build_nc(debug=False, npass=3, tiles=None):
    nc = bass.Bass("TRN2", target_bir_lowering=False)
    dt_in = lambda name, shape: nc.dram_tensor(name, shape, F32, kind="ExternalInput")
    x_d = dt_in("x", [NSEQ, SEQ, D])
    norm1_g = dt_in("norm1_g", [1, D]); w_in = dt_in("w_in", [1, D, INC])
    sg_ln_g = dt_in("sg_ln_g", [1, D]); sg_ln_b = dt_in("sg_ln_b", [1, D])
    sg_w = dt_in("sg_w", [1, 8, 128, 128]); sg_b = dt_in("sg_b", [1, 8, 128])
    w_a_out = dt_in("w_a_out", [1, D, D])
    cv_dw_w = dt_in("cv_dw_w", [1, 31, D]); cv_dw_b = dt_in("cv_dw_b", [1, D])
    cv_ln_g = dt_in("cv_ln_g", [1, D]); cv_ln_b = dt_in("cv_ln_b", [1, D])
    w_b_out = dt_in("w_b_out", [1, D, D]); w_o = dt_in("w_o", [1, D, D])
    norm2_g = dt_in("norm2_g", [1, D]); w_up = dt_in("w_up", [1, D, 2 * FFN])
    ffn_dw_w = dt_in("ffn_dw_w", [1, 3, FFN]); ffn_dw_b = dt_in("ffn_dw_b", [1, FFN])
    w_down = dt_in("w_down", [1, FFN, D]); final_g = dt_in("final_g", [D])
    out_d = nc.dram_tensor("out", [NSEQ, SEQ, D], F32, kind="ExternalOutput")
    skind = "ExternalOutput" if debug else "Internal"
    hgs = nc.dram_tensor("hgs", [NSEQ, D, SEQ + 30], BF16, kind=skind)
    hs = nc.dram_tensor("hs", [NSEQ, SEQ, D], F32, kind=skind)
    n2s = nc.dram_tensor("n2s", [NSEQ, D, SEQ + 2], BF16, kind=skind)
    wb_in = nc.dram_tensor("wb_in", [D, INC], BF16, kind="Internal")
    wb_a = nc.dram_tensor("wb_a", [D, D], BF16, kind="Internal")
    wb_b = nc.dram_tensor("wb_b", [D, D], BF16, kind="Internal")
    wb_o = nc.dram_tensor("wb_o", [D, D], BF16, kind="Internal")
    wb_up = nc.dram_tensor("wb_up", [D, 2 * FFN], BF16, kind="Internal")
    wb_dn = nc.dram_tensor("wb_dn", [FFN, D], BF16, kind="Internal")
    dgs = nc.dram_tensor("dgs", [8, 128, 31, 128], BF16, kind="Internal")

    import os
    SKIP = set(os.environ.get('KSKIP', '').split(','))
    NW = 5
    NTMP = 8
    with ExitStack() as es:
        def sb(name, shape, dt=F32):
            return es.enter_context(nc.sbuf_tensor(name, shape, dt))

        xbuf = [sb("xbuf%d" % i, [128, 4, D]) for i in range(2)]
        n_t = sb("n_t", [128, 4096], BF16)
        nT = sb("nT", [128, 8, T], BF16)
        bufB = sb("bufB", [128, 4096])
        bufC = sb("bufC", [128, 4096])
        arena = sb("arena", [128, 3, 4096], BF16)
        hgw = sb("hgw", [128, 8, HGW + 2], BF16)
        tmps = [sb("tmp%d" % i, [128, T + 2]) for i in range(NTMP)]
        dg = [sb("dg%d" % i, [128, 31, 128], BF16) for i in range(2)]
        lnt = [sb("lnt%d" % i, [128, T]) for i in range(2)]
        vbuf = [sb("vbuf%d" % i, [128, D]) for i in range(2)]
        wbuf = [sb("wbuf%d" % i, [128, 8, T], BF16) for i in range(NW)]
        ident_b = sb("ident_b", [128, 128], BF16); ident_f = sb("ident_f", [128, 128])
        ones_b = sb("ones_b", [128, 128], BF16); ones1_b = sb("ones1_b", [128, 128], BF16)
        VT = sb("VT", [128, 8, 38]); FT = sb("FT", [128, NFB, 4])
        wsT_b = sb("wsT_b", [128, 8, 128], BF16); Bh = sb("Bh", [128, 8, 128])
        fg_rep = Bh[:].rearrange("p h i -> p (h i)")
        mhalf = sb("mhalf", [128, 8])
        junk = sb("junk", [128, D], BF16)
        ss = sb("ss", [128, 8]); ms = sb("ms", [128, 8]); rstd = sb("rstd", [128, 8])
        bst = sb("bst", [128, 4, 2, 6]); mv = sb("mv", [128, 4, 2]); vr = sb("vr", [128, 4]); vrs = sb("vrs", [128, 4])
        zer = sb("zer", [128, 8, 16], BF16)
        pbs = [es.enter_context(nc.psum_tensor("pb%d" % i, [128, T], F32)) for i in range(8)]
        ptrs = [pbs[6].bitcast(BF16), pbs[7].bitcast(BF16)]

        vhat = arena[:, 0, :]
        yap = arena[:, 1, :]
        cbo = arena[:, 2, :]
        actf = arena[:].rearrange("p a b -> p (a b)")
        convb = bufC.bitcast(BF16)

        S = Sched(nc)
        st = {"bank": 0, "tmp": 0, "w": 0, "prep": 0}

        def bank():
            nb = st.get("nbank", 6)
            b = st["bank"] % nb; st["bank"] = (b + 1) % nb
            return pbs[b], ("pb%d" % b) if b < 6 else ("ptr%d" % (b - 6))

        def tmp():
            i = st["tmp"]; st["tmp"] = (i + 1) % NTMP
            return tmps[i], "tmp%d" % i

        prep_q = []

        def prep(src_ap, dst, name, rows, c0=0, c1=None, cs=0):
            for kb in range(rows // 128):
                prep_q.append((src_ap, dst, name, kb, c0, c1, cs))

        def prep_some(n):
            for _ in range(min(n, len(prep_q))):
                src_ap, dst, name, kb, c0, c1, cs = prep_q.pop(0)
                k = st["prep"]; st["prep"] = (k + 1) % 8
                c1 = dst.shape[1] if c1 is None else c1
                S.dma("pool", "prep%d" % k,
                      lambda e, kb=kb, dst=dst, src_ap=src_ap, c0=c0, c1=c1: e.dma_start(out=dst.ap()[kb * 128:(kb + 1) * 128, c0:c1],
                                                                                       in_=src_ap[kb * 128:(kb + 1) * 128, c0:c1]),
                      w=["%s:%d:%d" % (name, kb, cs)])
        prep(w_in.ap()[0], wb_in, "wb_in", D, 2048, 4096, 1)
        prep_some(8)
        if 'prep' in SKIP:
            prep_q.clear()
        prep(w_in.ap()[0], wb_in, "wb_in", D, 0, 2048, 0)
        prep(w_in.ap()[0], wb_in, "wb_in", D, 4096, 6144, 2)
        prep(w_a_out.ap()[0], wb_a, "wb_a", D)
        prep(w_b_out.ap()[0], wb_b, "wb_b", D)
        prep(w_o.ap()[0], wb_o, "wb_o", D)
        prep(w_up.ap()[0], wb_up, "wb_up", D)
        prep(w_down.ap()[0], wb_dn, "wb_dn", FFN)

        wbuf_x = [bufC.bitcast(BF16)[:, kk * 4096:(kk + 1) * 4096].rearrange("p (a b) -> p a b", a=8) for kk in range(2)]

        S.op("pool", lambda e: e.memset(ident_f[:], 0.0), w=["ident_f"])
        S.op("pool", lambda e: e.affine_select(out=ident_f[:], in_=ident_f[:], pattern=[[-1, 128]],
                                               compare_op=ALU.not_equal, fill=1.0, base=0, channel_multiplier=1),
             r=["ident_f"], w=["ident_f"])
        S.op("dve", lambda e: e.tensor_copy(out=ident_b[:], in_=ident_f[:]), r=["ident_f"], w=["ident_b"])
        S.op("pool", lambda e: e.memset(ones_b[:], 1.0 / D), w=["ones_b"])
        S.op("pool", lambda e: e.memset(ones1_b[:], 1.0), w=["ones1_b"])
        S.op("pool", lambda e: e.memset(mhalf[:], -0.5), w=["mhalf"])
        S.op("pool", lambda e: e.memset(zer[:], 0.0), w=["zer"])
        vrows = bufB[0:38, 0:D]
        frows = bufB[0:4, D:D + FFN]
        vec_list = [norm1_g.ap()[0], sg_ln_g.ap()[0], sg_ln_b.ap()[0], cv_dw_b.ap()[0], cv_ln_g.ap()[0],
                    cv_ln_b.ap()[0], norm2_g.ap()[0]]
        for i, v in enumerate(vec_list):
            S.dma("sp", "setup%d" % (i % 4), lambda e, i=i, v=v: e.dma_start(out=bufB[i:i + 1, 0:D], in_=v.rearrange("(o d) -> o d", o=1)),
                  w=["bufB"])
        S.dma("sp", "setup0", lambda e: e.dma_start(out=bufB[7:38, 0:D], in_=cv_dw_w.ap()[0]), w=["bufB"])
        S.dma("sp", "setup1", lambda e: e.dma_start(out=bufB[0:3, D:D + FFN], in_=ffn_dw_w.ap()[0]), w=["bufB"])
        S.dma("sp", "setup2", lambda e: e.dma_start(out=bufB[3:4, D:D + FFN], in_=ffn_dw_b.ap()[0].rearrange("(o d) -> o d", o=1)), w=["bufB"])
        for cb in range(0 if 'vec' in SKIP else 8):
            pb, pn = bank()
            S.op("pe", lambda e, cb=cb, pb=pb: e.transpose(out=pb[:, 0:38], in_=vrows[:, cb * 128:(cb + 1) * 128], identity=ident_f[0:38, 0:38]),
                 r=["bufB", "ident_f"], w=[pn])
            S.op("dve", lambda e, cb=cb, pb=pb: e.tensor_copy(out=VT[:, cb, :], in_=pb[:, 0:38]), r=[pn], w=["VT"])
        deferred = []

        def _d_ft():
            for fb in range(0 if 'vec' in SKIP else NFB):
                pb, pn = bank()
                S.op("pe", lambda e, fb=fb, pb=pb: e.transpose(out=pb[:, 0:4], in_=frows[:, fb * 128:(fb + 1) * 128], identity=ident_f[0:4, 0:4]),
                     r=["bufB", "ident_f"], w=[pn])
                S.op("dve", lambda e, fb=fb, pb=pb: e.tensor_copy(out=FT[:, fb, :], in_=pb[:, 0:4]), r=[pn], w=["FT"])
        deferred.append(_d_ft)

        swl = bufC[:, 0:1024].rearrange("p (h j) -> p h j", h=8)
        sgb_rep = bufC[:, 2048:3072]
        S.dma("sp", "setup3", lambda e: e.dma_start(out=swl, in_=sg_w.ap()[0].rearrange("h i j -> i h j")), w=["bufC"])
        S.dma("sp", "setup0", lambda e: e.dma_start(out=sgb_rep, in_=bass.AP(sg_b, 0, [[0, 128], [1, 1024]])), w=["bufC"])

        def _d_sg():
            swb = arena[:, 0, 0:1024].rearrange("p (h j) -> p h j", h=8)
            S.op("dve", lambda e: e.tensor_copy(out=swb, in_=swl), r=["bufC"], w=["vhat"])
            for g4 in range(0 if 'sgt' in SKIP else 2):
                pn = "ptr%d" % g4
                for hh in range(4):
                    hd = g4 * 4 + hh
                    S.op("pe", lambda e, hd=hd, hh=hh, g4=g4: e.transpose(out=ptrs[g4][:, hh * 128:(hh + 1) * 128], in_=swb[:, hd, :], identity=ident_b[:]),
                         r=["vhat", "ident_b"], w=[pn])
                S.op("act", lambda e, g4=g4: e.activation(out=wsT_b[:, g4 * 4:(g4 + 1) * 4, :].rearrange("p h i -> p (h i)"), in_=ptrs[g4][:, 0:512], func=AF.Copy),
                     r=[pn], w=["wsT_b"])
            for g4 in range(0 if 'rows' in SKIP else 2):
                pb, pn = bank()
                S.op("pe", lambda e, g4=g4, pb=pb: e.matmul(pb[:], lhsT=ones1_b[:], rhs=wsT_b[:, g4 * 4:(g4 + 1) * 4, :].rearrange("p h i -> p (h i)"), start=True, stop=True),
                     r=["wsT_b", "ones1_b"], w=[pn])
                for hh in range(4):
                    hd = g4 * 4 + hh
                    S.op("dve", lambda e, hd=hd, hh=hh, pb=pb: e.scalar_tensor_tensor(
                        out=Bh[:, hd, :], in0=pb[:, hh * 128:(hh + 1) * 128], scalar=VT[:, hd, 2:3],
                        in1=sgb_rep[:, hd * 128:(hd + 1) * 128], op0=ALU.mult, op1=ALU.add), r=[pn, "VT", "bufC"], w=["Bh"])
        deferred.append(_d_sg)

        def _d_diag(cb):
            dgi = cb % 2
            S.op("pool", lambda e: e.tensor_tensor(
                out=dg[dgi][:], in0=bass.AP(ident_f, 0, [[128, 128], [0, 31], [1, 128]]),
                in1=bass.AP(VT, cb * 38 + 7, [[8 * 38, 128], [1, 31], [0, 128]]), op=ALU.mult), r=["ident_f", "VT"], w=["dg%d" % dgi])
            S.dma("pool", "dgst%d" % dgi, lambda e: e.dma_start(out=dgs.ap()[cb], in_=dg[dgi][:]), r=["dg%d" % dgi], w=["dgs%d" % cb])
        def _d_zero():
            for q in range(0 if 'zero' in SKIP else NSEQ):
                hq = hgs.ap()[q].rearrange("(cb p) t -> p cb t", p=128)
                nq = n2s.ap()[q].rearrange("(cb p) t -> p cb t", p=128)
                S.dma("pool", "z0", lambda e, hq=hq: e.dma_start(out=hq[:, :, 0:15], in_=zer[:, :, 0:15]), r=["zer"], w=["hgz%d" % q])
                S.dma("pool", "z1", lambda e, hq=hq: e.dma_start(out=hq[:, :, SEQ + 15:SEQ + 30], in_=zer[:, :, 0:15]), r=["zer"], w=["hgz%d" % q])
                S.dma("pool", "z2", lambda e, nq=nq: e.dma_start(out=nq[:, :, 0:1], in_=zer[:, :, 0:1], allow_slow_non_contiguous=True), r=["zer"], w=["n2z%d" % q])
                S.dma("pool", "z3", lambda e, nq=nq: e.dma_start(out=nq[:, :, SEQ + 1:SEQ + 2], in_=zer[:, :, 0:1], allow_slow_non_contiguous=True), r=["zer"], w=["n2z%d" % q])

        deferred.append(_d_zero)
        for cb in range(8):
            deferred.append(lambda cb=cb: _d_diag(cb))

        def run_deferred(n):
            for _ in range(min(n, len(deferred))):
                deferred.pop(0)()

        def load_w(wt, name, kb0, nkb, c0, ncols=T):
            nring = st.get("nring", NW)
            s = st["w"] % nring; st["w"] = (s + 1) % nring
            if s >= NW:
                wb_ = wbuf_x[s - NW]
                src = wt.ap().rearrange("(kb p) c -> p kb c", p=128)[:, kb0:kb0 + nkb, c0:c0 + ncols]
                S.dma("sp", "w%d" % s, lambda e: e.dma_start(out=wb_[:, 0:nkb, 0:ncols], in_=src),
                      r=["%s:%d:%d" % (name, kb, (c0 // 2048) if name == "wb_in" else 0) for kb in range(kb0, kb0 + nkb)], w=["w%d" % s, "bufC"])
                return wb_, "w%d" % s
            src = wt.ap().rearrange("(kb p) c -> p kb c", p=128)[:, kb0:kb0 + nkb, c0:c0 + ncols]
            S.dma("sp", "w%d" % s, lambda e: e.dma_start(out=wbuf[s][:, 0:nkb, 0:ncols], in_=src),
                  r=["%s:%d:%d" % (name, kb, (c0 // 2048) if name == "wb_in" else 0) for kb in range(kb0, kb0 + nkb)], w=["w%d" % s])
            return wbuf[s], "w%d" % s

        def load_rows(dram3, q, s0, par):
            xt = xbuf[par]
            src = dram3.ap()[q, s0:s0 + T, :].rearrange("(j p) d -> p j d", p=128)
            return src

        def rms_rstd(xt, xn, col0):
            for j in range(4):
                S.op("act", lambda e, j=j: e.activation(out=junk[:], in_=xt[:, j, :], func=AF.Square, accum_out=ss[:, col0 + j:col0 + j + 1]),
                     r=[xn], w=["junk", "ss%d" % col0])
            S.op("dve", lambda e: e.tensor_scalar(out=ms[:, col0:col0 + 4], in0=ss[:, col0:col0 + 4], scalar1=1.0 / D, scalar2=EPS, op0=ALU.mult, op1=ALU.add),
                 r=["ss%d" % col0], w=["ms%d" % col0])
            S.op("pool", lambda e: e.tensor_tensor(out=rstd[:, col0:col0 + 4], in0=ms[:, col0:col0 + 4], in1=mhalf[:, 0:4], op=ALU.pow),
                 r=["ms%d" % col0, "mhalf"], w=["rstd%d" % col0])

        def norm_scale(xt, xn, col0):
            for j in range(4):
                if j % 2 == 0:
                    S.op("dve", lambda e, j=j: e.tensor_scalar(out=n_t[:, j * 1024:(j + 1) * 1024], in0=xt[:, j, :], scalar1=rstd[:, col0 + j:col0 + j + 1],
                                                               scalar2=None, op0=ALU.mult),
                         r=[xn, "rstd%d" % col0], w=["n_t%d" % j])
                else:
                    S.op("act", lambda e, j=j: e.activation(out=n_t[:, j * 1024:(j + 1) * 1024], in_=xt[:, j, :], func=AF.Copy, scale=rstd[:, col0 + j:col0 + j + 1]),
                         r=[xn, "rstd%d" % col0], w=["n_t%d" % j])

        def norm_T(gidx, dst=None, dstn="nT"):
            dst = nT if dst is None else dst
            for kb in range(8):
                half = kb % 2
                pn = "ptr%d" % half
                for j in range(4):
                    S.op("pe", lambda e, kb=kb, j=j, half=half: e.transpose(
                        out=ptrs[half][:, j * 128:(j + 1) * 128],
                        in_=n_t[:, j * 1024 + kb * 128: j * 1024 + (kb + 1) * 128], identity=ident_b[:]),
                        r=["n_t%d" % j, "ident_b"], w=[pn])
                if kb % 2 == 0:
                    S.op("act", lambda e, kb=kb, half=half: e.activation(out=dst[:, kb, 0:T], in_=ptrs[half][:, 0:512], func=AF.Copy,
                                                                         scale=VT[:, kb, gidx:gidx + 1]),
                         r=[pn, "VT"], w=[dstn + str(kb)])
                else:
                    S.op("dve", lambda e, kb=kb, half=half: e.tensor_scalar(out=dst[:, kb, 0:T], in0=ptrs[half][:, 0:512],
                                                                            scalar1=VT[:, kb, gidx:gidx + 1], scalar2=None, op0=ALU.mult),
                         r=[pn, "VT"], w=[dstn + str(kb)])

        def mm_fm(pb, pn, wslot, wn, cbl, rhs_fn, rhs_names, nkb=8, n0=0, n1=T, rhs_shift=0):
            for kb in range(nkb):
                S.op("pe", lambda e, kb=kb: e.matmul(pb[:, n0:n1], lhsT=wslot[:, kb, cbl * 128:(cbl + 1) * 128], rhs=rhs_fn(kb),
                                                     start=(kb == 0), stop=(kb == nkb - 1)),
                     r=[wn] + [(x[:-1] + str(kb)) if x.endswith("*") else x for x in rhs_names], w=[pn])

        tile_id = [0]

        def xinfo(q, i, k):
            par = k % 2
            return xbuf[par], "xbuf%d" % par

        def p1_front(q, i, k):
            xt, xn = xinfo(q, i, k)
            S.dma("sp", xn, lambda e: e.dma_start(out=xt[:], in_=load_rows(x_d, q, i * T, 0)), w=[xn])
            rms_rstd(xt, xn, 0)
            norm_scale(xt, xn, 0)

        def p1_main(q, i, k):
            s0 = i * T
            hg_st, hgn = (yap, "yap") if k % 2 == 0 else (cbo, "cbo")
            for q4 in range(2):
                sa, san = load_w(wb_in, "wb_in", 0, 8, 2048 + T * q4)
                sg, sgn = load_w(wb_in, "wb_in", 0, 8, 3072 + T * q4)
                for cbl in range(4):
                    cb = q4 * 4 + cbl
                    pa, pan = bank()
                    mm_fm(pa, pan, sa, san, cbl, lambda kb: nT[:, kb, :], ["nT*"])
                    pg, pgn = bank()
                    mm_fm(pg, pgn, sg, sgn, cbl, lambda kb: nT[:, kb, :], ["nT*"])
                    t1, t1n = tmp()
                    S.op("act", lambda e, pg=pg, t1=t1: e.activation(out=t1[:, 0:T], in_=pg[:], func=AF.Sigmoid), r=[pgn], w=[t1n])
                    S.op("dve", lambda e, pa=pa, t1=t1, cb=cb: e.tensor_tensor(out=hg_st[:, cb * T:(cb + 1) * T], in0=pa[:], in1=t1[:, 0:T], op=ALU.mult),
                         r=[pan, t1n], w=[hgn])
            dst = hgs.ap()[q].rearrange("(cb p) t -> p cb t", p=128)[:, :, 15 + s0:15 + s0 + T]
            S.dma("pool", "st_" + hgn, lambda e: e.dma_start(out=dst, in_=hg_st.rearrange("p (cb t) -> p cb t", cb=8)), r=[hgn], w=["hg%d:%d" % (q, i)])

        def run_pass1(tl):
            n = len(tl)
            for k, (q, i) in enumerate(tl):
                if k == 0:
                    p1_front(q, i, k)
                    norm_T(0)
                if k + 1 < n:
                    p1_front(tl[k + 1][0], tl[k + 1][1], k + 1)
                prep_some(3)
                p1_main(q, i, k)
                run_deferred(2 if k == 0 else 1)
                if k + 1 < n:
                    norm_T(0)
            prep_some(max(0, len(prep_q) - 30))
            run_deferred(1000)

        def p2_load(q, i, k):
            xt, xn = xinfo(q, i, k)
            s0 = i * T
            S.dma("sp", xn, lambda e: e.dma_start(out=xt[:], in_=load_rows(x_d, q, s0, 0)), w=[xn])
            src = hgs.ap()[q].rearrange("(cb p) t -> p cb t", p=128)[:, :, s0:s0 + HGW]
            S.dma("sp", "hgw", lambda e: e.dma_start(out=hgw[:, :, 0:HGW], in_=src),
                  r=["hgz%d" % q] + ["hg%d:%d" % (q, kk) for kk in (i - 1, i, i + 1) if 0 <= kk < NT], w=["hgw%d" % kk_ for kk_ in range(8)])
            p2_taps_enqueue()
            dg_load(0)
            dg_load(1)

        KD = 10

        tapq = []

        def p2_taps_enqueue():
            first = not st.get("cvf_started", False)
            st["cvf_started"] = True
            k0 = 31 - KD
            for cb in range(8):
                dst = bufB[:, cb * T:(cb + 1) * T]
                wn = ["cvf%d" % cb] + (["bufB"] if first else [])
                tapq.append((cb, lambda cb=cb, dst=dst, wn=wn: S.op("dve", lambda e: e.tensor_scalar(
                    out=dst, in0=hgw[:, cb, k0:k0 + T], scalar1=VT[:, cb, 7 + k0:8 + k0], scalar2=VT[:, cb, 3:4],
                    op0=ALU.mult, op1=ALU.add), r=["hgw%d" % cb, "VT"], w=wn)))
                for kk in range(k0 + 1, 31):
                    tapq.append((cb, lambda cb=cb, kk=kk, dst=dst: S.op("dve", lambda e: e.scalar_tensor_tensor(
                        out=dst, in0=hgw[:, cb, kk:kk + T], scalar=VT[:, cb, 7 + kk:8 + kk], in1=dst,
                        op0=ALU.mult, op1=ALU.add), r=["hgw%d" % cb, "VT", "cvf%d" % cb], w=["cvf%d" % cb])))

        def tap_some(n, upto=None):
            while tapq and n > 0 and (upto is None or tapq[0][0] <= upto):
                tapq.pop(0)[1]()
                n -= 1

        def dg_load(cb):
            dgi = cb % 2
            k0 = 31 - KD
            S.dma("sp", "dg%d" % dgi, lambda e: e.dma_start(out=dg[dgi][:, 0:k0, :], in_=dgs.ap()[cb, :, 0:k0, :]), r=["dgs%d" % cb], w=["dg%d" % dgi])

        def p2_conv(q, i, k, cbs=range(8)):
            k0 = 31 - KD
            tap_some(10 ** 6, upto=max(cbs))
            for cb in cbs:
                dgi = cb % 2
                dgn = "dg%d" % dgi
                dst = bufB[:, cb * T:(cb + 1) * T]
                pc, pcn = bank()
                for kk in range(k0):
                    S.op("pe", lambda e, cb=cb, kk=kk, pc=pc, dgi=dgi: e.matmul(pc[:], lhsT=dg[dgi][:, kk, :], rhs=hgw[:, cb, kk:kk + T], start=(kk == 0), stop=(kk == k0 - 1)),
                         r=[dgn, "hgw%d" % cb], w=[pcn])
                if cb + 2 < 8:
                    dg_load(cb + 2)
                S.op("dve", lambda e, pc=pc, dst=dst: e.tensor_tensor(out=dst, in0=pc[:], in1=dst, op=ALU.add), r=[pcn, "cvf%d" % cb], w=["cvf%d" % cb])
                S.op("act", lambda e, cb=cb, dst=dst: e.activation(out=convb[:, 4096 + cb * T: 4096 + (cb + 1) * T], in_=dst, func=AF.Square),
                     r=["cvf%d" % cb], w=["bufC"])
                S.op("pool", lambda e, cb=cb, dst=dst: e.tensor_copy(out=convb[:, cb * T:(cb + 1) * T], in_=dst), r=["cvf%d" % cb], w=["bufC"])

        def p2_lnstats_mm():
            pmean, pmeann = bank()
            for cb in range(8):
                S.op("pe", lambda e, cb=cb: e.matmul(pmean[:], lhsT=ones_b[:], rhs=convb[:, cb * T:(cb + 1) * T], start=(cb == 0), stop=(cb == 7)),
                     r=["ones_b", "bufC"], w=[pmeann])
            pmsq, pmsqn = bank()
            for cb in range(8):
                S.op("pe", lambda e, cb=cb: e.matmul(pmsq[:], lhsT=ones_b[:], rhs=convb[:, 4096 + cb * T: 4096 + (cb + 1) * T], start=(cb == 0), stop=(cb == 7)),
                     r=["ones_b", "bufC"], w=[pmsqn])
            return pmean, pmeann, pmsq, pmsqn

        def p2_lnstats_chain(pmean, pmeann, pmsq, pmsqn):
            mean_sb, meann = lnt[0], "lnt0"; rsb, rsbn = lnt[1], "lnt1"
            m2, m2n = tmp()
            S.op("act", lambda e: e.activation(out=mean_sb[:, 0:T], in_=pmean[:], func=AF.Copy), r=[pmeann], w=[meann])
            S.op("dve", lambda e: e.tensor_tensor(out=m2[:, 0:T], in0=mean_sb[:, 0:T], in1=mean_sb[:, 0:T], op=ALU.mult), r=[meann], w=[m2n])
            S.op("dve", lambda e: e.scalar_tensor_tensor(out=m2[:, 0:T], in0=pmsq[:], scalar=EPS, in1=m2[:, 0:T], op0=ALU.add, op1=ALU.subtract),
                 r=[pmsqn, m2n], w=[m2n])
            S.op("act", lambda e: e.activation(out=m2[:, 0:T], in_=m2[:, 0:T], func=AF.Sqrt), r=[m2n], w=[m2n])
            S.op("dve", lambda e: e.reciprocal(out=rsb[:, 0:T], in_=m2[:, 0:T]), r=[m2n], w=[rsbn])

        def p2_lnnorm(cb):
            mean_sb, meann = lnt[0], "lnt0"; rsb, rsbn = lnt[1], "lnt1"
            d1, d1n = tmp()
            S.op("dve", lambda e: e.tensor_tensor(out=d1[:, 0:T], in0=bufB[:, cb * T:(cb + 1) * T], in1=mean_sb[:, 0:T], op=ALU.subtract),
                 r=["cvf%d" % cb, meann], w=[d1n])
            S.op("dve", lambda e: e.tensor_tensor(out=d1[:, 0:T], in0=d1[:, 0:T], in1=rsb[:, 0:T], op=ALU.mult), r=[d1n, rsbn], w=[d1n])
            S.op("act", lambda e: e.activation(out=cbo[:, cb * T:(cb + 1) * T], in_=d1[:, 0:T], func=AF.Silu,
                                               scale=VT[:, cb, 4:5], bias=VT[:, cb, 5:6]), r=[d1n, "VT"], w=["cbo"])

        def p2_v():
            sv0, sv0n = load_w(wb_in, "wb_in", 0, 8, 1024)
            sv1, sv1n = load_w(wb_in, "wb_in", 0, 8, 1024 + T)
            for j in range(4):
                vb, vbn = vbuf[j % 2], "vbuf%d" % (j % 2)
                for half, (sv, svn) in enumerate(((sv0, sv0n), (sv1, sv1n))):
                    pv, pvn = bank()
                    for kb in range(8):
                        S.op("pe", lambda e, kb=kb, j=j, pv=pv, sv=sv: e.matmul(pv[:], lhsT=nT[:, kb, j * 128:(j + 1) * 128], rhs=sv[:, kb, :],
                                                                              start=(kb == 0), stop=(kb == 7)), r=[svn, "nT%d" % kb], w=[pvn])
                    S.op("act", lambda e, pv=pv, vb=vb, half=half: e.activation(out=vb[:, half * T:(half + 1) * T], in_=pv[:], func=AF.Gelu_apprx_tanh),
                         r=[pvn], w=[vbn])
                    S.op("dve", lambda e, j=j, half=half, vb=vb: e.bn_stats(out=bst[:, j, half, :], in_=vb[:, half * T:(half + 1) * T]), r=[vbn], w=["bst%d" % j])
                S.op("dve", lambda e, j=j: e.bn_aggr(out=mv[:, j, :], in_=bst[:, j, :, :].rearrange("p a b -> p (a b)")), r=["bst%d" % j], w=["mv%d" % j])
                S.op("dve", lambda e, j=j: e.tensor_scalar(out=vr[:, j:j + 1], in0=mv[:, j, 1:2], scalar1=EPS, scalar2=None, op0=ALU.add), r=["mv%d" % j], w=["vr%d" % j])
                S.op("pool", lambda e, j=j: e.tensor_tensor(out=vrs[:, j:j + 1], in0=vr[:, j:j + 1], in1=mhalf[:, 0:1], op=ALU.pow), r=["vr%d" % j, "mhalf"], w=["vrs%d" % j])
                S.op("dve", lambda e, j=j, vb=vb: e.tensor_scalar(out=vhat[:, j * 1024:(j + 1) * 1024], in0=vb[:],
                                                                  scalar1=mv[:, j, 0:1], scalar2=vrs[:, j:j + 1], op0=ALU.subtract, op1=ALU.mult),
                     r=[vbn, "mv%d" % j, "vrs%d" % j], w=["vhat"])

        def p2_umix():
            pend = []

            def mix(hd, ut, utn):
                pm, pmn = bank()
                for j in range(4):
                    S.op("pe", lambda e, j=j: e.matmul(pm[:, j * 128:(j + 1) * 128], lhsT=vhat[:, j * 1024 + hd * 128: j * 1024 + (hd + 1) * 128],
                                                       rhs=wsT_b[:, hd, :], start=True, stop=True), r=["vhat", "wsT_b"], w=[pmn])
                t1, t1n = tmp()
                bh_bc = bass.AP(Bh, hd * 128, [[1024, 128], [0, 4], [1, 128]])
                S.op("dve", lambda e: e.scalar_tensor_tensor(
                    out=t1[:, 0:T].rearrange("p (a b) -> p a b", a=4), in0=pm[:].rearrange("p (a b) -> p a b", a=4),
                    scalar=VT[:, hd, 1:2], in1=bh_bc, op0=ALU.mult, op1=ALU.add), r=[pmn, "VT", "Bh"], w=[t1n])
                S.op("pool", lambda e: e.tensor_tensor(out=yap[:, hd * T:(hd + 1) * T], in0=t1[:, 0:T], in1=ut[:, 0:T], op=ALU.mult),
                     r=[t1n, utn], w=["yap"])

            for q4 in range(2):
                su, sun = load_w(wb_in, "wb_in", 0, 8, T * q4)
                for cbl in range(4):
                    hd = q4 * 4 + cbl
                    pu, pun = bank()
                    mm_fm(pu, pun, su, sun, cbl, lambda kb: nT[:, kb, :], ["nT*"])
                    ut, utn = tmp()
                    S.op("act", lambda e, pu=pu, ut=ut: e.activation(out=ut[:, 0:T], in_=pu[:], func=AF.Gelu_apprx_tanh), r=[pun], w=[utn])
                    pend.append((hd, ut, utn))
                    if len(pend) > 3:
                        mix(*pend.pop(0))
            while pend:
                mix(*pend.pop(0))
            for cb in range(8):
                p2_lnnorm(cb)

        def p2_outproj():
            merged = n_t
            for q4 in range(2):
                sGA, sGAn = load_w(wb_in, "wb_in", 0, 8, 4096 + T * q4)
                sGB, sGBn = load_w(wb_in, "wb_in", 0, 8, 5120 + T * q4)
                sA, sAn = load_w(wb_a, "wb_a", 0, 8, T * q4)
                sB, sBn = load_w(wb_b, "wb_b", 0, 8, T * q4)
                pend = []

                def front(cbl):
                    pga, pgan = bank()
                    mm_fm(pga, pgan, sGA, sGAn, cbl, lambda kb: nT[:, kb, :], ["nT*"])
                    ga, gan = tmp(); ma, man = tmp()
                    S.op("act", lambda e: e.activation(out=ga[:, 0:T], in_=pga[:], func=AF.Sigmoid), r=[pgan], w=[gan])
                    pgb, pgbn = bank()
                    mm_fm(pgb, pgbn, sGB, sGBn, cbl, lambda kb: nT[:, kb, :], ["nT*"])
                    gb, gbn = tmp()
                    S.op("act", lambda e: e.activation(out=gb[:, 0:T], in_=pgb[:], func=AF.Sigmoid), r=[pgbn], w=[gbn])
                    pya, pyan = bank()
                    mm_fm(pya, pyan, sA, sAn, cbl, lambda kb: yap[:, kb * T:(kb + 1) * T], ["yap"])
                    S.op("dve", lambda e: e.tensor_tensor(out=ma[:, 0:T], in0=pya[:], in1=ga[:, 0:T], op=ALU.mult), r=[pyan, gan], w=[man])
                    tap_some(4)
                    return (cbl, ma, man, gb, gbn)

                def back(cbl, ma, man, gb, gbn):
                    ob = q4 * 4 + cbl
                    pyb, pybn = bank()
                    mm_fm(pyb, pybn, sB, sBn, cbl, lambda kb: cbo[:, kb * T:(kb + 1) * T], ["cbo"])
                    S.op("dve", lambda e: e.tensor_tensor(out=gb[:, 0:T], in0=pyb[:], in1=gb[:, 0:T], op=ALU.mult), r=[pybn, gbn], w=[gbn])
                    tap_some(4)
                    S.op("pool", lambda e: e.tensor_tensor(out=merged[:, ob * T:(ob + 1) * T], in0=ma[:, 0:T], in1=gb[:, 0:T], op=ALU.add),
                         r=[man, gbn], w=["n_t%d" % (ob // 2)])

                for cbl in range(4):
                    pend.append(front(cbl))
                    if len(pend) > 1:
                        back(*pend.pop(0))
                while pend:
                    back(*pend.pop(0))

        def p2_wo(q, i, k):
            merged = n_t
            xt, xn = xinfo(q, i, k)
            for half in range(2):
                so, son = load_w(wb_o, "wb_o", 0, 8, T * half)
                for j in range(4):
                    ph, phn = bank()
                    for kb in range(8):
                        S.op("pe", lambda e, kb=kb, j=j, ph=ph, so=so: e.matmul(ph[:], lhsT=merged[:, kb * T + j * 128: kb * T + (j + 1) * 128], rhs=so[:, kb, :],
                                                                              start=(kb == 0), stop=(kb == 7)), r=[son, "n_t%d" % (kb // 2)], w=[phn])
                    S.op("dve", lambda e, j=j, half=half, ph=ph: e.tensor_tensor(out=xt[:, j, half * T:(half + 1) * T], in0=ph[:], in1=xt[:, j, half * T:(half + 1) * T], op=ALU.add),
                         r=[phn, xn], w=[xn])
                    tap_some(2)
            dsth = hs.ap()[q, i * T:(i + 1) * T, :].rearrange("(j p) d -> p j d", p=128)
            S.dma("pool", "st_" + xn, lambda e: e.dma_start(out=dsth, in_=xt[:]), r=[xn], w=["h%d:%d" % (q, i)])

        def p2_norm2_b(q, i, k):
            norm_T(6, hgw, "hgw")
            dstn = n2s.ap()[q].rearrange("(cb p) t -> p cb t", p=128)[:, :, 1 + i * T:1 + (i + 1) * T]
            S.dma("pool", "st_nT", lambda e: e.dma_start(out=dstn, in_=hgw[:, :, 0:T]), r=["hgw%d" % kk_ for kk_ in range(8)], w=["n2%d:%d" % (q, i)])

        def run_pass2(tl):
            n = len(tl)
            for k, (q, i) in enumerate(tl):
                xt, xn = xinfo(q, i, k)
                nxt = k + 1 < n
                if k == 0:
                    p2_load(q, i, k)
                    rms_rstd(xt, xn, 0)
                    p2_conv(q, i, k)
                    norm_scale(xt, xn, 0)
                stt_ = p2_lnstats_mm()
                norm_T(0)
                p2_lnstats_chain(*stt_)
                p2_v()
                p2_umix()
                if nxt:
                    q2, i2 = tl[k + 1]
                    xt2, xn2 = xinfo(q2, i2, k + 1)
                    p2_load(q2, i2, k + 1)
                p2_outproj()
                if nxt:
                    rms_rstd(xt2, xn2, 0)
                    p2_conv(q2, i2, k + 1, range(0, 4))
                prep_some(2)
                p2_wo(q, i, k)
                rms_rstd(xt, xn, 4)
                norm_scale(xt, xn, 4)
                if nxt:
                    p2_conv(q2, i2, k + 1, range(4, 8))
                p2_norm2_b(q, i, k)
                if nxt:
                    norm_scale(xt2, xn2, 0)

        def pass3(q, i, hook=None):
            par = tile_id[0] % 2; tile_id[0] += 1
            s0 = i * T
            xt, xn = xbuf[par], "xbuf%d" % par
            n2w = hgw
            S.dma("sp", xn, lambda e: e.dma_start(out=xt[:], in_=load_rows(hs, q, s0, par)), r=["h%d:%d" % (q, i)], w=[xn])
            src = n2s.ap()[q].rearrange("(cb p) t -> p cb t", p=128)[:, :, s0:s0 + N2W]
            S.dma("sp", "hgw", lambda e: e.dma_start(out=n2w[:, :, 0:N2W], in_=src),
                  r=["n2z%d" % q] + ["n2%d:%d" % (q, k) for k in (i - 1, i, i + 1) if 0 <= k < NT], w=["hgw%d" % kk_ for kk_ in range(8)])
            arena_names = ["vhat", "yap", "cbo"]
            first_tile = not st.get("p3_started", False)
            st["p3_started"] = True
            for c4 in range(6):
                nfb = 4 if c4 < 5 else 2
                sg_, sgn = load_w(wb_up, "wb_up", 0, 8, c4 * T, nfb * 128)
                sv_, svn = load_w(wb_up, "wb_up", 0, 8, FFN + c4 * T, nfb * 128)
                for cbl in range(nfb):
                    fb = c4 * 4 + cbl
                    pgA, pgAn = bank()
                    mm_fm(pgA, pgAn, sg_, sgn, cbl, lambda kb: n2w[:, kb, 0:HN], ["hgw*"], n0=0, n1=HN)
                    pgB, pgBn = bank()
                    mm_fm(pgB, pgBn, sg_, sgn, cbl, lambda kb: n2w[:, kb, HN:N2W], ["hgw*"], n0=0, n1=HN)
                    pv, pvn = bank()
                    mm_fm(pv, pvn, sv_, svn, cbl, lambda kb: n2w[:, kb, 1:T + 1], ["hgw*"])
                    gs, gsn = tmp(); c1, c1n = tmp(); ge, gen = tmp()
                    S.op("act", lambda e, pgA=pgA, gs=gs: e.activation(out=gs[:, 0:HN], in_=pgA[:, 0:HN], func=AF.Copy), r=[pgAn], w=[gsn])
                    S.op("act", lambda e, pgB=pgB, gs=gs: e.activation(out=gs[:, HN:N2W], in_=pgB[:, 0:HN], func=AF.Copy), r=[pgBn], w=[gsn])
                    S.op("dve", lambda e, fb=fb, gs=gs, c1=c1: e.tensor_scalar(out=c1[:, 0:T], in0=gs[:, 1:T + 1], scalar1=FT[:, fb, 1:2], scalar2=FT[:, fb, 3:4],
                                                                            op0=ALU.mult, op1=ALU.add), r=[gsn, "FT"], w=[c1n])
                    S.op("dve", lambda e, fb=fb, gs=gs, c1=c1: e.scalar_tensor_tensor(out=c1[:, 0:T], in0=gs[:, 0:T], scalar=FT[:, fb, 0:1], in1=c1[:, 0:T],
                                                                                   op0=ALU.mult, op1=ALU.add), r=[gsn, c1n, "FT"], w=[c1n])
                    S.op("dve", lambda e, fb=fb, gs=gs, c1=c1: e.scalar_tensor_tensor(out=c1[:, 0:T], in0=gs[:, 2:T + 2], scalar=FT[:, fb, 2:3], in1=c1[:, 0:T],
                                                                                   op0=ALU.mult, op1=ALU.add), r=[gsn, c1n, "FT"], w=[c1n])
                    S.op("act", lambda e, c1=c1, ge=ge: e.activation(out=ge[:, 0:T], in_=c1[:, 0:T], func=AF.Gelu_apprx_tanh), r=[c1n], w=[gen])
                    S.op("dve", lambda e, fb=fb, ge=ge, pv=pv: e.tensor_tensor(out=actf[:, fb * T:(fb + 1) * T], in0=pv[:], in1=ge[:, 0:T], op=ALU.mult),
                         r=[pvn, gen], w=["act%d" % fb] + (arena_names if first_tile else []))
                    if hook and fb >= 3:
                        hook.pop(0)()
            for half in range(2):
                phs = [bank() for _ in range(4)]
                for (kb0, nkb) in ((0, 8), (8, 8), (16, 6)):
                    sd, sdn = load_w(wb_dn, "wb_dn", kb0, nkb, T * half)
                    for j in range(4):
                        ph, phn = phs[j]
                        for kk in range(nkb):
                            fb = kb0 + kk
                            S.op("pe", lambda e, fb=fb, kk=kk, j=j, ph=ph, sd=sd: e.matmul(ph[:], lhsT=actf[:, fb * T + j * 128: fb * T + (j + 1) * 128], rhs=sd[:, kk, :],
                                                                                         start=(fb == 0), stop=(fb == NFB - 1)), r=[sdn, "act%d" % fb], w=[phn])
                for j in range(4):
                    ph, phn = phs[j]
                    S.op("dve", lambda e, j=j, half=half, ph=ph: e.tensor_tensor(out=xt[:, j, half * T:(half + 1) * T], in0=ph[:], in1=xt[:, j, half * T:(half + 1) * T], op=ALU.add),
                         r=[phn, xn], w=[xn])
            def final_steps():
                steps = []

                def sq(j, last):
                    def f():
                        S.op("act", lambda e: e.activation(out=junk[:], in_=xt[:, j, :], func=AF.Square, accum_out=ss[:, j:j + 1]),
                             r=[xn], w=["junk", "ss0"])
                        if last:
                            S.op("dve", lambda e: e.tensor_scalar(out=ms[:, 0:4], in0=ss[:, 0:4], scalar1=1.0 / D, scalar2=EPS, op0=ALU.mult, op1=ALU.add),
                                 r=["ss0"], w=["ms0"])
                            S.op("pool", lambda e: e.tensor_tensor(out=rstd[:, 0:4], in0=ms[:, 0:4], in1=mhalf[:, 0:4], op=ALU.pow),
                                 r=["ms0", "mhalf"], w=["rstd0"])
                    return f

                def sc(j, last):
                    def f():
                        extra = [] if st.get("fin_started") else (["bufB"] + ["cvf%d" % c_ for c_ in range(8)])
                        S.op("act", lambda e: e.activation(out=bufB[:, j * 1024:(j + 1) * 1024], in_=xt[:, j, :], func=AF.Copy, scale=rstd[:, j:j + 1]),
                             r=[xn, "rstd0"], w=["fin%d" % j] + extra)
                        S.op("pool", lambda e: e.tensor_tensor(out=bufB[:, j * 1024:(j + 1) * 1024], in0=bufB[:, j * 1024:(j + 1) * 1024], in1=fg_rep, op=ALU.mult),
                             r=["fin%d" % j, "Bh"], w=["fin%d" % j])
                        if last:
                            st["fin_started"] = True
                            dsto = out_d.ap()[q, s0:s0 + T, :].rearrange("(j p) d -> p j d", p=128)
                            S.dma("pool", "st_bufB", lambda e: e.dma_start(out=dsto, in_=bufB[:].rearrange("p (j d) -> p j d", j=4)),
                                  r=["fin%d" % jj for jj in range(4)], w=["out%d:%d" % (q, i)])
                    return f
                for j in range(4):
                    steps.append(sq(j, j == 3))
                for j in range(4):
                    steps.append(sc(j, j == 3))
                return steps
            return final_steps()

        if tiles is None:
            tiles = [(q, i) for q in range(NSEQ) for i in range(NT)]
            tl = [tiles, tiles, tiles]
        else:
            tl = [[(0, i) for i in range(min(NT, tiles + 2 - k))] for k in range(3)]
        if npass >= 1:
            run_pass1(tl[0])
        else:
            prep_some(1000)
            run_deferred(1000)
        if npass >= 2:
            run_pass2(tl[1])
        prep_some(1000)
        if npass >= 3:
            S.dma("sp", "setup1", lambda e: e.dma_start(out=fg_rep[:], in_=bass.AP(final_g, 0, [[0, 128], [1, D]])), w=["Bh"])
            st["nring"] = NW + 2
            st["nbank"] = 8
            pend3 = None
            for (q, i) in tl[2]:
                pend3 = pass3(q, i, pend3)
            while pend3:
                pend3.pop(0)()

        keys = ["pe", "act", "dve", "pool"] + ["dma:" + k for k in S.dma_keys()]
        sems = {k: es.enter_context(nc.semaphore(k.replace(":", "_"))) for k in keys}
        cnt = S.emit(sems)
        for k, v in cnt.items():
            if k.startswith("dma:"):
                nc.sync.wait_ge(sems[k], v)
        print("ops:", len(S.ops), "sem counts:", {k: v for k, v in cnt.items() if not k.startswith("dma:")})
    return nc


_NC_CACHE = {}


def kernel(x, norm1_g, w_in, sg_ln_g, sg_ln_b, sg_w, sg_b, w_a_out, cv_dw_w, cv_dw_b,
           cv_ln_g, cv_ln_b, w_b_out, w_o, norm2_g, w_up, ffn_dw_w, ffn_dw_b, w_down, final_g):
    n = 8
    f = lambda a: np.ascontiguousarray(np.asarray(a, dtype=np.float32))
    x = f(x)
    shared = dict(norm1_g=f(norm1_g), w_in=f(w_in), sg_ln_g=f(sg_ln_g), sg_ln_b=f(sg_ln_b), sg_w=f(sg_w), sg_b=f(sg_b),
                  w_a_out=f(w_a_out), cv_dw_w=f(cv_dw_w), cv_dw_b=f(cv_dw_b), cv_ln_g=f(cv_ln_g), cv_ln_b=f(cv_ln_b),
                  w_b_out=f(w_b_out), w_o=f(w_o), norm2_g=f(norm2_g), w_up=f(w_up), ffn_dw_w=f(ffn_dw_w),
                  ffn_dw_b=f(ffn_dw_b), w_down=f(w_down), final_g=f(final_g))
    if "nc" not in _NC_CACHE:
        _NC_CACHE["nc"] = build_nc()
    nc = _NC_CACHE["nc"]
    in_maps = [dict(shared, x=x[NSEQ * c:NSEQ * (c + 1)]) for c in range(n)]
    res = run_bass_kernel_spmd(nc, in_maps, core_ids=list(range(n)))
    return np.concatenate([np.asarray(r["out"]) for r in res.results], axis=0).astype(np.float32)
```

```python
import numpy as np
from contextlib import ExitStack
import concourse.bass as bass
import concourse.mybir as mybir
from concourse.bass_utils import run_bass_kernel_spmd

F32 = mybir.dt.float32
BF16 = mybir.dt.bfloat16
AF = mybir.ActivationFunctionType
ALU = mybir.AluOpType

D = 1024
SEQ = 4096
NSEQ = 2
T = 512
NT = SEQ // T
FFN = 2816
NFB = FFN // 128
INC = 6144
EPS = 1e-6
HGW = T + 30
N2W = T + 2
HN = N2W // 2


class Sched:
    def __init__(self, nc):
        self.nc = nc
        self.ops = []
        self.eng = {"pe": nc.tensor, "act": nc.scalar, "dve": nc.vector,
                    "pool": nc.gpsimd, "sp": nc.sync}

    def op(self, eng, fn, r=(), w=()):
        self.ops.append((eng, fn, tuple(r), tuple(w)))

    def dma(self, queue, key, fn, r=(), w=()):
        self.ops.append((("dma", queue, key), fn, tuple(r), tuple(w)))

    def dma_keys(self):
        return sorted({e[2] for e, _, _, _ in self.ops if isinstance(e, tuple)})

    def emit(self, sems):
        ops = self.ops
        n = len(ops)
        last_w, readers, last_key = {}, {}, {}
        deps = [None] * n
        needed = [False] * n
        for i, (eng, fn, r, w) in enumerate(ops):
            d = set()
            for x in r:
                if x in last_w:
                    d.add(last_w[x])
            for x in w:
                if x in last_w:
                    d.add(last_w[x])
                d.update(readers.get(x, ()))
            if isinstance(eng, tuple):
                if eng[2] in last_key:
                    d.add(last_key[eng[2]])
                last_key[eng[2]] = i
            d.discard(i)
            if eng == "pe":
                d = {j for j in d if ops[j][0] != "pe"}
            deps[i] = d
            for j in d:
                needed[j] = True
            for x in r:
                readers.setdefault(x, []).append(i)
            for x in w:
                last_w[x] = i
                readers[x] = []
        cnt = {}
        sig = [None] * n
        for i, (eng, fn, r, w) in enumerate(ops):
            if isinstance(eng, tuple):
                key = "dma:" + eng[2]
                cnt[key] = cnt.get(key, 0) + 16
                sig[i] = (key, cnt[key])
            elif needed[i]:
                cnt[eng] = cnt.get(eng, 0) + 1
                sig[i] = (eng, cnt[eng])
        waited = {}
        for i, (eng, fn, r, w) in enumerate(ops):
            issuer = eng[1] if isinstance(eng, tuple) else eng
            e = self.eng[issuer]
            wd = waited.setdefault(issuer, {})
            need = {}
            for j in deps[i]:
                s, v = sig[j]
                if v > need.get(s, 0):
                    need[s] = v
            for s, v in need.items():
                if wd.get(s, 0) < v:
                    e.wait_ge(sems[s], v)
                    wd[s] = v
            inst = fn(e)
            if sig[i] is not None:
                inst.then_inc(sems[sig[i][0]], 16 if isinstance(eng, tuple) else 1)
        return cnt


def build_nc(debug=False, npass=3, tiles=None):
    nc = bass.Bass("TRN2", target_bir_lowering=False)
    dt_in = lambda name, shape: nc.dram_tensor(name, shape, F32, kind="ExternalInput")
    x_d = dt_in("x", [NSEQ, SEQ, D])
    norm1_g = dt_in("norm1_g", [1, D]); w_in = dt_in("w_in", [1, D, INC])
    sg_ln_g = dt_in("sg_ln_g", [1, D]); sg_ln_b = dt_in("sg_ln_b", [1, D])
    sg_w = dt_in("sg_w", [1, 8, 128, 128]); sg_b = dt_in("sg_b", [1, 8, 128])
    w_a_out = dt_in("w_a_out", [1, D, D])
    cv_dw_w = dt_in("cv_dw_w", [1, 31, D]); cv_dw_b = dt_in("cv_dw_b", [1, D])
    cv_ln_g = dt_in("cv_ln_g", [1, D]); cv_ln_b = dt_in("cv_ln_b", [1, D])
    w_b_out = dt_in("w_b_out", [1, D, D]); w_o = dt_in("w_o", [1, D, D])
    norm2_g = dt_in("norm2_g", [1, D]); w_up = dt_in("w_up", [1, D, 2 * FFN])
    ffn_dw_w = dt_in("ffn_dw_w", [1, 3, FFN]); ffn_dw_b = dt_in("ffn_dw_b", [1, FFN])
    w_down = dt_in("w_down", [1, FFN, D]); final_g = dt_in("final_g", [D])
    out_d = nc.dram_tensor("out", [NSEQ, SEQ, D], F32, kind="ExternalOutput")
    skind = "ExternalOutput" if debug else "Internal"
    hgs = nc.dram_tensor("hgs", [NSEQ, D, SEQ + 30], BF16, kind=skind)
    hs = nc.dram_tensor("hs", [NSEQ, SEQ, D], F32, kind=skind)
    n2s = nc.dram_tensor("n2s", [NSEQ, D, SEQ + 2], BF16, kind=skind)
    wb_in = nc.dram_tensor("wb_in", [D, INC], BF16, kind="Internal")
    wb_a = nc.dram_tensor("wb_a", [D, D], BF16, kind="Internal")
    wb_b = nc.dram_tensor("wb_b", [D, D], BF16, kind="Internal")
    wb_o = nc.dram_tensor("wb_o", [D, D], BF16, kind="Internal")
    wb_up = nc.dram_tensor("wb_up", [D, 2 * FFN], BF16, kind="Internal")
    wb_dn = nc.dram_tensor("wb_dn", [FFN, D], BF16, kind="Internal")
    dgs = nc.dram_tensor("dgs", [8, 128, 31, 128], BF16, kind="Internal")

    import os
    SKIP = set(os.environ.get('KSKIP', '').split(','))
    NW = 5
    NTMP = 8
    with ExitStack() as es:
        def sb(name, shape, dt=F32):
            return es.enter_context(nc.sbuf_tensor(name, shape, dt))

        xbuf = [sb("xbuf%d" % i, [128, 4, D]) for i in range(2)]
        n_t = sb("n_t", [128, 4096], BF16)
        nT = sb("nT", [128, 8, T], BF16)
        bufB = sb("bufB", [128, 4096])
        bufC = sb("bufC", [128, 4096])
        arena = sb("arena", [128, 3, 4096], BF16)
        hgw = sb("hgw", [128, 8, HGW + 2], BF16)
        tmps = [sb("tmp%d" % i, [128, T + 2]) for i in range(NTMP)]
        dg = [sb("dg%d" % i, [128, 31, 128], BF16) for i in range(2)]
        lnt = [sb("lnt%d" % i, [128, T]) for i in range(2)]
        vbuf = [sb("vbuf%d" % i, [128, D]) for i in range(2)]
        wbuf = [sb("wbuf%d" % i, [128, 8, T], BF16) for i in range(NW)]
        ident_b = sb("ident_b", [128, 128], BF16); ident_f = sb("ident_f", [128, 128])
        ones_b = sb("ones_b", [128, 128], BF16); ones1_b = sb("ones1_b", [128, 128], BF16)
        VT = sb("VT", [128, 8, 38]); FT = sb("FT", [128, NFB, 4])
        wsT_b = sb("wsT_b", [128, 8, 128], BF16); Bh = sb("Bh", [128, 8, 128])
        fg_rep = Bh[:].rearrange("p h i -> p (h i)")
        mhalf = sb("mhalf", [128, 8])
        junk = sb("junk", [128, D], BF16)
        ss = sb("ss", [128, 8]); ms = sb("ms", [128, 8]); rstd = sb("rstd", [128, 8])
        bst = sb("bst", [128, 4, 2, 6]); mv = sb("mv", [128, 4, 2]); vr = sb("vr", [128, 4]); vrs = sb("vrs", [128, 4])
        zer = sb("zer", [128, 8, 16], BF16)
        pbs = [es.enter_context(nc.psum_tensor("pb%d" % i, [128, T], F32)) for i in range(8)]
        ptrs = [pbs[6].bitcast(BF16), pbs[7].bitcast(BF16)]

        vhat = arena[:, 0, :]
        yap = arena[:, 1, :]
        cbo = arena[:, 2, :]
        actf = arena[:].rearrange("p a b -> p (a b)")
        convb = bufC.bitcast(BF16)

        S = Sched(nc)
        st = {"bank": 0, "tmp": 0, "w": 0, "prep": 0}

        def bank():
            nb = st.get("nbank", 6)
            b = st["bank"] % nb; st["bank"] = (b + 1) % nb
            return pbs[b], ("pb%d" % b) if b < 6 else ("ptr%d" % (b - 6))

        def tmp():
            i = st["tmp"]; st["tmp"] = (i + 1) % NTMP
            return tmps[i], "tmp%d" % i

        prep_q = []

        def prep(src_ap, dst, name, rows, c0=0, c1=None, cs=0):
            for kb in range(rows // 128):
                prep_q.append((src_ap, dst, name, kb, c0, c1, cs))

        def prep_some(n):
            for _ in range(min(n, len(prep_q))):
                src_ap, dst, name, kb, c0, c1, cs = prep_q.pop(0)
                k = st["prep"]; st["prep"] = (k + 1) % 8
                c1 = dst.shape[1] if c1 is None else c1
                S.dma("pool", "prep%d" % k,
                      lambda e, kb=kb, dst=dst, src_ap=src_ap, c0=c0, c1=c1: e.dma_start(out=dst.ap()[kb * 128:(kb + 1) * 128, c0:c1],
                                                                                       in_=src_ap[kb * 128:(kb + 1) * 128, c0:c1]),
                      w=["%s:%d:%d" % (name, kb, cs)])
        prep(w_in.ap()[0], wb_in, "wb_in", D, 2048, 4096, 1)
        prep_some(8)
        if 'prep' in SKIP:
            prep_q.clear()
        prep(w_in.ap()[0], wb_in, "wb_in", D, 0, 2048, 0)
        prep(w_in.ap()[0], wb_in, "wb_in", D, 4096, 6144, 2)
        prep(w_a_out.ap()[0], wb_a, "wb_a", D)
        prep(w_b_out.ap()[0], wb_b, "wb_b", D)
        prep(w_o.ap()[0], wb_o, "wb_o", D)
        prep(w_up.ap()[0], wb_up, "wb_up", D)
        prep(w_down.ap()[0], wb_dn, "wb_dn", FFN)

        wbuf_x = [bufC.bitcast(BF16)[:, kk * 4096:(kk + 1) * 4096].rearrange("p (a b) -> p a b", a=8) for kk in range(2)]

        S.op("pool", lambda e: e.memset(ident_f[:], 0.0), w=["ident_f"])
        S.op("pool", lambda e: e.affine_select(out=ident_f[:], in_=ident_f[:], pattern=[[-1, 128]],
                                               compare_op=ALU.not_equal, fill=1.0, base=0, channel_multiplier=1),
             r=["ident_f"], w=["ident_f"])
        S.op("dve", lambda e: e.tensor_copy(out=ident_b[:], in_=ident_f[:]), r=["ident_f"], w=["ident_b"])
        S.op("pool", lambda e: e.memset(ones_b[:], 1.0 / D), w=["ones_b"])
        S.op("pool", lambda e: e.memset(ones1_b[:], 1.0), w=["ones1_b"])
        S.op("pool", lambda e: e.memset(mhalf[:], -0.5), w=["mhalf"])
        S.op("pool", lambda e: e.memset(zer[:], 0.0), w=["zer"])
        vrows = bufB[0:38, 0:D]
        frows = bufB[0:4, D:D + FFN]
        vec_list = [norm1_g.ap()[0], sg_ln_g.ap()[0], sg_ln_b.ap()[0], cv_dw_b.ap()[0], cv_ln_g.ap()[0],
                    cv_ln_b.ap()[0], norm2_g.ap()[0]]
        for i, v in enumerate(vec_list):
            S.dma("sp", "setup%d" % (i % 4), lambda e, i=i, v=v: e.dma_start(out=bufB[i:i + 1, 0:D], in_=v.rearrange("(o d) -> o d", o=1)),
                  w=["bufB"])
        S.dma("sp", "setup0", lambda e: e.dma_start(out=bufB[7:38, 0:D], in_=cv_dw_w.ap()[0]), w=["bufB"])
        S.dma("sp", "setup1", lambda e: e.dma_start(out=bufB[0:3, D:D + FFN], in_=ffn_dw_w.ap()[0]), w=["bufB"])
        S.dma("sp", "setup2", lambda e: e.dma_start(out=bufB[3:4, D:D + FFN], in_=ffn_dw_b.ap()[0].rearrange("(o d) -> o d", o=1)), w=["bufB"])
        for cb in range(0 if 'vec' in SKIP else 8):
            pb, pn = bank()
            S.op("pe", lambda e, cb=cb, pb=pb: e.transpose(out=pb[:, 0:38], in_=vrows[:, cb * 128:(cb + 1) * 128], identity=ident_f[0:38, 0:38]),
                 r=["bufB", "ident_f"], w=[pn])
            S.op("dve", lambda e, cb=cb, pb=pb: e.tensor_copy(out=VT[:, cb, :], in_=pb[:, 0:38]), r=[pn], w=["VT"])
        deferred = []

        def _d_ft():
            for fb in range(0 if 'vec' in SKIP else NFB):
                pb, pn = bank()
                S.op("pe", lambda e, fb=fb, pb=pb: e.transpose(out=pb[:, 0:4], in_=frows[:, fb * 128:(fb + 1) * 128], identity=ident_f[0:4, 0:4]),
                     r=["bufB", "ident_f"], w=[pn])
                S.op("dve", lambda e, fb=fb, pb=pb: e.tensor_copy(out=FT[:, fb, :], in_=pb[:, 0:4]), r=[pn], w=["FT"])
        deferred.append(_d_ft)

        swl = bufC[:, 0:1024].rearrange("p (h j) -> p h j", h=8)
        sgb_rep = bufC[:, 2048:3072]
        S.dma("sp", "setup3", lambda e: e.dma_start(out=swl, in_=sg_w.ap()[0].rearrange("h i j -> i h j")), w=["bufC"])
        S.dma("sp", "setup0", lambda e: e.dma_start(out=sgb_rep, in_=bass.AP(sg_b, 0, [[0, 128], [1, 1024]])), w=["bufC"])

        def _d_sg():
            swb = arena[:, 0, 0:1024].rearrange("p (h j) -> p h j", h=8)
            S.op("dve", lambda e: e.tensor_copy(out=swb, in_=swl), r=["bufC"], w=["vhat"])
            for g4 in range(0 if 'sgt' in SKIP else 2):
                pn = "ptr%d" % g4
                for hh in range(4):
                    hd = g4 * 4 + hh
                    S.op("pe", lambda e, hd=hd, hh=hh, g4=g4: e.transpose(out=ptrs[g4][:, hh * 128:(hh + 1) * 128], in_=swb[:, hd, :], identity=ident_b[:]),
                         r=["vhat", "ident_b"], w=[pn])
                S.op("act", lambda e, g4=g4: e.activation(out=wsT_b[:, g4 * 4:(g4 + 1) * 4, :].rearrange("p h i -> p (h i)"), in_=ptrs[g4][:, 0:512], func=AF.Copy),
                     r=[pn], w=["wsT_b"])
            for g4 in range(0 if 'rows' in SKIP else 2):
                pb, pn = bank()
                S.op("pe", lambda e, g4=g4, pb=pb: e.matmul(pb[:], lhsT=ones1_b[:], rhs=wsT_b[:, g4 * 4:(g4 + 1) * 4, :].rearrange("p h i -> p (h i)"), start=True, stop=True),
                     r=["wsT_b", "ones1_b"], w=[pn])
                for hh in range(4):
                    hd = g4 * 4 + hh
                    S.op("dve", lambda e, hd=hd, hh=hh, pb=pb: e.scalar_tensor_tensor(
                        out=Bh[:, hd, :], in0=pb[:, hh * 128:(hh + 1) * 128], scalar=VT[:, hd, 2:3],
                        in1=sgb_rep[:, hd * 128:(hd + 1) * 128], op0=ALU.mult, op1=ALU.add), r=[pn, "VT", "bufC"], w=["Bh"])
        deferred.append(_d_sg)

        def _d_diag(cb):
            dgi = cb % 2
            S.op("pool", lambda e: e.tensor_tensor(
                out=dg[dgi][:], in0=bass.AP(ident_f, 0, [[128, 128], [0, 31], [1, 128]]),
                in1=bass.AP(VT, cb * 38 + 7, [[8 * 38, 128], [1, 31], [0, 128]]), op=ALU.mult), r=["ident_f", "VT"], w=["dg%d" % dgi])
            S.dma("pool", "dgst%d" % dgi, lambda e: e.dma_start(out=dgs.ap()[cb], in_=dg[dgi][:]), r=["dg%d" % dgi], w=["dgs%d" % cb])
        def _d_zero():
            for q in range(0 if 'zero' in SKIP else NSEQ):
                hq = hgs.ap()[q].rearrange("(cb p) t -> p cb t", p=128)
                nq = n2s.ap()[q].rearrange("(cb p) t -> p cb t", p=128)
                S.dma("pool", "z0", lambda e, hq=hq: e.dma_start(out=hq[:, :, 0:15], in_=zer[:, :, 0:15]), r=["zer"], w=["hgz%d" % q])
                S.dma("pool", "z1", lambda e, hq=hq: e.dma_start(out=hq[:, :, SEQ + 15:SEQ + 30], in_=zer[:, :, 0:15]), r=["zer"], w=["hgz%d" % q])
                S.dma("pool", "z2", lambda e, nq=nq: e.dma_start(out=nq[:, :, 0:1], in_=zer[:, :, 0:1], allow_slow_non_contiguous=True), r=["zer"], w=["n2z%d" % q])
                S.dma("pool", "z3", lambda e, nq=nq: e.dma_start(out=nq[:, :, SEQ + 1:SEQ + 2], in_=zer[:, :, 0:1], allow_slow_non_contiguous=True), r=["zer"], w=["n2z%d" % q])

        deferred.append(_d_zero)
        for cb in range(8):
            deferred.append(lambda cb=cb: _d_diag(cb))

        def run_deferred(n):
            for _ in range(min(n, len(deferred))):
                deferred.pop(0)()

        def load_w(wt, name, kb0, nkb, c0, ncols=T):
            nring = st.get("nring", NW)
            s = st["w"] % nring; st["w"] = (s + 1) % nring
            if s >= NW:
                wb_ = wbuf_x[s - NW]
                src = wt.ap().rearrange("(kb p) c -> p kb c", p=128)[:, kb0:kb0 + nkb, c0:c0 + ncols]
                S.dma("sp", "w%d" % s, lambda e: e.dma_start(out=wb_[:, 0:nkb, 0:ncols], in_=src),
                      r=["%s:%d:%d" % (name, kb, (c0 // 2048) if name == "wb_in" else 0) for kb in range(kb0, kb0 + nkb)], w=["w%d" % s, "bufC"])
                return wb_, "w%d" % s
            src = wt.ap().rearrange("(kb p) c -> p kb c", p=128)[:, kb0:kb0 + nkb, c0:c0 + ncols]
            S.dma("sp", "w%d" % s, lambda e: e.dma_start(out=wbuf[s][:, 0:nkb, 0:ncols], in_=src),
                  r=["%s:%d:%d" % (name, kb, (c0 // 2048) if name == "wb_in" else 0) for kb in range(kb0, kb0 + nkb)], w=["w%d" % s])
            return wbuf[s], "w%d" % s

        def load_rows(dram3, q, s0, par):
            xt = xbuf[par]
            src = dram3.ap()[q, s0:s0 + T, :].rearrange("(j p) d -> p j d", p=128)
            return src

        def rms_rstd(xt, xn, col0):
            for j in range(4):
                S.op("act", lambda e, j=j: e.activation(out=junk[:], in_=xt[:, j, :], func=AF.Square, accum_out=ss[:, col0 + j:col0 + j + 1]),
                     r=[xn], w=["junk", "ss%d" % col0])
            S.op("dve", lambda e: e.tensor_scalar(out=ms[:, col0:col0 + 4], in0=ss[:, col0:col0 + 4], scalar1=1.0 / D, scalar2=EPS, op0=ALU.mult, op1=ALU.add),
                 r=["ss%d" % col0], w=["ms%d" % col0])
            S.op("pool", lambda e: e.tensor_tensor(out=rstd[:, col0:col0 + 4], in0=ms[:, col0:col0 + 4], in1=mhalf[:, 0:4], op=ALU.pow),
                 r=["ms%d" % col0, "mhalf"], w=["rstd%d" % col0])

        def norm_scale(xt, xn, col0):
            for j in range(4):
                if j % 2 == 0:
                    S.op("dve", lambda e, j=j: e.tensor_scalar(out=n_t[:, j * 1024:(j + 1) * 1024], in0=xt[:, j, :], scalar1=rstd[:, col0 + j:col0 + j + 1],
                                                               scalar2=None, op0=ALU.mult),
                         r=[xn, "rstd%d" % col0], w=["n_t%d" % j])
                else:
                    S.op("act", lambda e, j=j: e.activation(out=n_t[:, j * 1024:(j + 1) * 1024], in_=xt[:, j, :], func=AF.Copy, scale=rstd[:, col0 + j:col0 + j + 1]),
                         r=[xn, "rstd%d" % col0], w=["n_t%d" % j])

        def norm_T(gidx, dst=None, dstn="nT"):
            dst = nT if dst is None else dst
            for kb in range(8):
                half = kb % 2
                pn = "ptr%d" % half
                for j in range(4):
                    S.op("pe", lambda e, kb=kb, j=j, half=half: e.transpose(
                        out=ptrs[half][:, j * 128:(j + 1) * 128],
                        in_=n_t[:, j * 1024 + kb * 128: j * 1024 + (kb + 1) * 128], identity=ident_b[:]),
                        r=["n_t%d" % j, "ident_b"], w=[pn])
                if kb % 2 == 0:
                    S.op("act", lambda e, kb=kb, half=half: e.activation(out=dst[:, kb, 0:T], in_=ptrs[half][:, 0:512], func=AF.Copy,
                                                                         scale=VT[:, kb, gidx:gidx + 1]),
                         r=[pn, "VT"], w=[dstn + str(kb)])
                else:
                    S.op("dve", lambda e, kb=kb, half=half: e.tensor_scalar(out=dst[:, kb, 0:T], in0=ptrs[half][:, 0:512],
                                                                            scalar1=VT[:, kb, gidx:gidx + 1], scalar2=None, op0=ALU.mult),
                         r=[pn, "VT"], w=[dstn + str(kb)])

        def mm_fm(pb, pn, wslot, wn, cbl, rhs_fn, rhs_names, nkb=8, n0=0, n1=T, rhs_shift=0):
            for kb in range(nkb):
                S.op("pe", lambda e, kb=kb: e.matmul(pb[:, n0:n1], lhsT=wslot[:, kb, cbl * 128:(cbl + 1) * 128], rhs=rhs_fn(kb),
                                                     start=(kb == 0), stop=(kb == nkb - 1)),
                     r=[wn] + [(x[:-1] + str(kb)) if x.endswith("*") else x for x in rhs_names], w=[pn])

        tile_id = [0]

        def xinfo(q, i, k):
            par = k % 2
            return xbuf[par], "xbuf%d" % par

        def p1_front(q, i, k):
            xt, xn = xinfo(q, i, k)
            S.dma("sp", xn, lambda e: e.dma_start(out=xt[:], in_=load_rows(x_d, q, i * T, 0)), w=[xn])
            rms_rstd(xt, xn, 0)
            norm_scale(xt, xn, 0)

        def p1_main(q, i, k):
            s0 = i * T
            hg_st, hgn = (yap, "yap") if k % 2 == 0 else (cbo, "cbo")
            for q4 in range(2):
                sa, san = load_w(wb_in, "wb_in", 0, 8, 2048 + T * q4)
                sg, sgn = load_w(wb_in, "wb_in", 0, 8, 3072 + T * q4)
                for cbl in range(4):
                    cb = q4 * 4 + cbl
                    pa, pan = bank()
                    mm_fm(pa, pan, sa, san, cbl, lambda kb: nT[:, kb, :], ["nT*"])
                    pg, pgn = bank()
                    mm_fm(pg, pgn, sg, sgn, cbl, lambda kb: nT[:, kb, :], ["nT*"])
                    t1, t1n = tmp()
                    S.op("act", lambda e, pg=pg, t1=t1: e.activation(out=t1[:, 0:T], in_=pg[:], func=AF.Sigmoid), r=[pgn], w=[t1n])
                    S.op("dve", lambda e, pa=pa, t1=t1, cb=cb: e.tensor_tensor(out=hg_st[:, cb * T:(cb + 1) * T], in0=pa[:], in1=t1[:, 0:T], op=ALU.mult),
                         r=[pan, t1n], w=[hgn])
            dst = hgs.ap()[q].rearrange("(cb p) t -> p cb t", p=128)[:, :, 15 + s0:15 + s0 + T]
            S.dma("pool", "st_" + hgn, lambda e: e.dma_start(out=dst, in_=hg_st.rearrange("p (cb t) -> p cb t", cb=8)), r=[hgn], w=["hg%d:%d" % (q, i)])

        def run_pass1(tl):
            n = len(tl)
            for k, (q, i) in enumerate(tl):
                if k == 0:
                    p1_front(q, i, k)
                    norm_T(0)
                if k + 1 < n:
                    p1_front(tl[k + 1][0], tl[k + 1][1], k + 1)
                prep_some(3)
                p1_main(q, i, k)
                run_deferred(2 if k == 0 else 1)
                if k + 1 < n:
                    norm_T(0)
            prep_some(max(0, len(prep_q) - 30))
            run_deferred(1000)

        def p2_load(q, i, k):
            xt, xn = xinfo(q, i, k)
            s0 = i * T
            S.dma("sp", xn, lambda e: e.dma_start(out=xt[:], in_=load_rows(x_d, q, s0, 0)), w=[xn])
            src = hgs.ap()[q].rearrange("(cb p) t -> p cb t", p=128)[:, :, s0:s0 + HGW]
            S.dma("sp", "hgw", lambda e: e.dma_start(out=hgw[:, :, 0:HGW], in_=src),
                  r=["hgz%d" % q] + ["hg%d:%d" % (q, kk) for kk in (i - 1, i, i + 1) if 0 <= kk < NT], w=["hgw%d" % kk_ for kk_ in range(8)])
            p2_taps_enqueue()

        KD = 10

        tapq = []

        def p2_taps_enqueue():
            first = not st.get("cvf_started", False)
            st["cvf_started"] = True
            k0 = 31 - (1 if first else KD)
            st["k0"] = k0
            for cb in range(8):
                dst = bufB[:, cb * T:(cb + 1) * T]
                wn = ["cvf%d" % cb] + (["bufB"] if first else [])
                tapq.append((cb, lambda cb=cb, dst=dst, wn=wn: S.op("dve", lambda e: e.tensor_scalar(
                    out=dst, in0=hgw[:, cb, k0:k0 + T], scalar1=VT[:, cb, 7 + k0:8 + k0], scalar2=VT[:, cb, 3:4],
                    op0=ALU.mult, op1=ALU.add), r=["hgw%d" % cb, "VT"], w=wn)))
                for kk in range(k0 + 1, 31):
                    tapq.append((cb, lambda cb=cb, kk=kk, dst=dst: S.op("dve", lambda e: e.scalar_tensor_tensor(
                        out=dst, in0=hgw[:, cb, kk:kk + T], scalar=VT[:, cb, 7 + kk:8 + kk], in1=dst,
                        op0=ALU.mult, op1=ALU.add), r=["hgw%d" % cb, "VT", "cvf%d" % cb], w=["cvf%d" % cb])))

        def tap_some(n, upto=None):
            while tapq and n > 0 and (upto is None or tapq[0][0] <= upto):
                tapq.pop(0)[1]()
                n -= 1

        def dg_load(cb):
            dgi = cb % 2
            k0 = st["k0"]
            S.dma("sp", "dg%d" % dgi, lambda e: e.dma_start(out=dg[dgi][:, 0:k0, :], in_=dgs.ap()[cb, :, 0:k0, :]), r=["dgs%d" % cb], w=["dg%d" % dgi])

        def p2_conv(q, i, k, cbs=range(8)):
            k0 = st["k0"]
            tap_some(10 ** 6, upto=max(cbs))
            for cb in cbs:
                dgi = cb % 2
                dgn = "dg%d" % dgi
                dst = bufB[:, cb * T:(cb + 1) * T]
                dg_load(cb)
                pc, pcn = bank()
                for kk in range(k0):
                    S.op("pe", lambda e, cb=cb, kk=kk, pc=pc, dgi=dgi: e.matmul(pc[:], lhsT=dg[dgi][:, kk, :], rhs=hgw[:, cb, kk:kk + T], start=(kk == 0), stop=(kk == k0 - 1)),
                         r=[dgn, "hgw%d" % cb], w=[pcn])
                S.op("dve", lambda e, pc=pc, dst=dst: e.tensor_tensor(out=dst, in0=pc[:], in1=dst, op=ALU.add), r=[pcn, "cvf%d" % cb], w=["cvf%d" % cb])
                S.op("act", lambda e, cb=cb, dst=dst: e.activation(out=convb[:, 4096 + cb * T: 4096 + (cb + 1) * T], in_=dst, func=AF.Square),
                     r=["cvf%d" % cb], w=["bufC"])
                S.op("pool", lambda e, cb=cb, dst=dst: e.tensor_copy(out=convb[:, cb * T:(cb + 1) * T], in_=dst), r=["cvf%d" % cb], w=["bufC"])

        def p2_lnstats_mm():
            pmean, pmeann = bank()
            for cb in range(8):
                S.op("pe", lambda e, cb=cb: e.matmul(pmean[:], lhsT=ones_b[:], rhs=convb[:, cb * T:(cb + 1) * T], start=(cb == 0), stop=(cb == 7)),
                     r=["ones_b", "bufC"], w=[pmeann])
            pmsq, pmsqn = bank()
            for cb in range(8):
                S.op("pe", lambda e, cb=cb: e.matmul(pmsq[:], lhsT=ones_b[:], rhs=convb[:, 4096 + cb * T: 4096 + (cb + 1) * T], start=(cb == 0), stop=(cb == 7)),
                     r=["ones_b", "bufC"], w=[pmsqn])
            return pmean, pmeann, pmsq, pmsqn

        def p2_lnstats_chain(pmean, pmeann, pmsq, pmsqn):
            mean_sb, meann = lnt[0], "lnt0"; rsb, rsbn = lnt[1], "lnt1"
            m2, m2n = tmp()
            S.op("act", lambda e: e.activation(out=mean_sb[:, 0:T], in_=pmean[:], func=AF.Copy), r=[pmeann], w=[meann])
            S.op("dve", lambda e: e.tensor_tensor(out=m2[:, 0:T], in0=mean_sb[:, 0:T], in1=mean_sb[:, 0:T], op=ALU.mult), r=[meann], w=[m2n])
            S.op("dve", lambda e: e.scalar_tensor_tensor(out=m2[:, 0:T], in0=pmsq[:], scalar=EPS, in1=m2[:, 0:T], op0=ALU.add, op1=ALU.subtract),
                 r=[pmsqn, m2n], w=[m2n])
            S.op("act", lambda e: e.activation(out=m2[:, 0:T], in_=m2[:, 0:T], func=AF.Sqrt), r=[m2n], w=[m2n])
            S.op("dve", lambda e: e.reciprocal(out=rsb[:, 0:T], in_=m2[:, 0:T]), r=[m2n], w=[rsbn])

        def p2_lnnorm(cb):
            mean_sb, meann = lnt[0], "lnt0"; rsb, rsbn = lnt[1], "lnt1"
            d1, d1n = tmp()
            S.op("dve", lambda e: e.tensor_tensor(out=d1[:, 0:T], in0=bufB[:, cb * T:(cb + 1) * T], in1=mean_sb[:, 0:T], op=ALU.subtract),
                 r=["cvf%d" % cb, meann], w=[d1n])
            S.op("dve", lambda e: e.tensor_tensor(out=d1[:, 0:T], in0=d1[:, 0:T], in1=rsb[:, 0:T], op=ALU.mult), r=[d1n, rsbn], w=[d1n])
            S.op("act", lambda e: e.activation(out=cbo[:, cb * T:(cb + 1) * T], in_=d1[:, 0:T], func=AF.Silu,
                                               scale=VT[:, cb, 4:5], bias=VT[:, cb, 5:6]), r=[d1n, "VT"], w=["cbo"])

        def p2_v():
            sv0, sv0n = load_w(wb_in, "wb_in", 0, 8, 1024)
            sv1, sv1n = load_w(wb_in, "wb_in", 0, 8, 1024 + T)
            for j in range(4):
                vb, vbn = vbuf[j % 2], "vbuf%d" % (j % 2)
                for half, (sv, svn) in enumerate(((sv0, sv0n), (sv1, sv1n))):
                    pv, pvn = bank()
                    for kb in range(8):
                        S.op("pe", lambda e, kb=kb, j=j, pv=pv, sv=sv: e.matmul(pv[:], lhsT=nT[:, kb, j * 128:(j + 1) * 128], rhs=sv[:, kb, :],
                                                                              start=(kb == 0), stop=(kb == 7)), r=[svn, "nT%d" % kb], w=[pvn])
                    S.op("act", lambda e, pv=pv, vb=vb, half=half: e.activation(out=vb[:, half * T:(half + 1) * T], in_=pv[:], func=AF.Gelu_apprx_tanh),
                         r=[pvn], w=[vbn])
                    S.op("dve", lambda e, j=j, half=half, vb=vb: e.bn_stats(out=bst[:, j, half, :], in_=vb[:, half * T:(half + 1) * T]), r=[vbn], w=["bst%d" % j])
                S.op("dve", lambda e, j=j: e.bn_aggr(out=mv[:, j, :], in_=bst[:, j, :, :].rearrange("p a b -> p (a b)")), r=["bst%d" % j], w=["mv%d" % j])
                S.op("dve", lambda e, j=j: e.tensor_scalar(out=vr[:, j:j + 1], in0=mv[:, j, 1:2], scalar1=EPS, scalar2=None, op0=ALU.add), r=["mv%d" % j], w=["vr%d" % j])
                S.op("pool", lambda e, j=j: e.tensor_tensor(out=vrs[:, j:j + 1], in0=vr[:, j:j + 1], in1=mhalf[:, 0:1], op=ALU.pow), r=["vr%d" % j, "mhalf"], w=["vrs%d" % j])
                S.op("dve", lambda e, j=j, vb=vb: e.tensor_scalar(out=vhat[:, j * 1024:(j + 1) * 1024], in0=vb[:],
                                                                  scalar1=mv[:, j, 0:1], scalar2=vrs[:, j:j + 1], op0=ALU.subtract, op1=ALU.mult),
                     r=[vbn, "mv%d" % j, "vrs%d" % j], w=["vhat"])

        def p2_umix():
            pend = []

            def mix(hd, ut, utn):
                pm, pmn = bank()
                for j in range(4):
                    S.op("pe", lambda e, j=j: e.matmul(pm[:, j * 128:(j + 1) * 128], lhsT=vhat[:, j * 1024 + hd * 128: j * 1024 + (hd + 1) * 128],
                                                       rhs=wsT_b[:, hd, :], start=True, stop=True), r=["vhat", "wsT_b"], w=[pmn])
                t1, t1n = tmp()
                bh_bc = bass.AP(Bh, hd * 128, [[1024, 128], [0, 4], [1, 128]])
                S.op("dve", lambda e: e.scalar_tensor_tensor(
                    out=t1[:, 0:T].rearrange("p (a b) -> p a b", a=4), in0=pm[:].rearrange("p (a b) -> p a b", a=4),
                    scalar=VT[:, hd, 1:2], in1=bh_bc, op0=ALU.mult, op1=ALU.add), r=[pmn, "VT", "Bh"], w=[t1n])
                S.op("pool", lambda e: e.tensor_tensor(out=yap[:, hd * T:(hd + 1) * T], in0=t1[:, 0:T], in1=ut[:, 0:T], op=ALU.mult),
                     r=[t1n, utn], w=["yap"])

            for q4 in range(2):
                su, sun = load_w(wb_in, "wb_in", 0, 8, T * q4)
                for cbl in range(4):
                    hd = q4 * 4 + cbl
                    pu, pun = bank()
                    mm_fm(pu, pun, su, sun, cbl, lambda kb: nT[:, kb, :], ["nT*"])
                    ut, utn = tmp()
                    S.op("act", lambda e, pu=pu, ut=ut: e.activation(out=ut[:, 0:T], in_=pu[:], func=AF.Gelu_apprx_tanh), r=[pun], w=[utn])
                    pend.append((hd, ut, utn))
                    if len(pend) > 3:
                        mix(*pend.pop(0))
            while pend:
                mix(*pend.pop(0))
            for cb in range(8):
                p2_lnnorm(cb)

        def p2_outproj():
            merged = n_t
            for q4 in range(2):
                sGA, sGAn = load_w(wb_in, "wb_in", 0, 8, 4096 + T * q4)
                sGB, sGBn = load_w(wb_in, "wb_in", 0, 8, 5120 + T * q4)
                sA, sAn = load_w(wb_a, "wb_a", 0, 8, T * q4)
                sB, sBn = load_w(wb_b, "wb_b", 0, 8, T * q4)
                pend = []

                def front(cbl):
                    pga, pgan = bank()
                    mm_fm(pga, pgan, sGA, sGAn, cbl, lambda kb: nT[:, kb, :], ["nT*"])
                    ga, gan = tmp(); ma, man = tmp()
                    S.op("act", lambda e: e.activation(out=ga[:, 0:T], in_=pga[:], func=AF.Sigmoid), r=[pgan], w=[gan])
                    pgb, pgbn = bank()
                    mm_fm(pgb, pgbn, sGB, sGBn, cbl, lambda kb: nT[:, kb, :], ["nT*"])
                    gb, gbn = tmp()
                    S.op("act", lambda e: e.activation(out=gb[:, 0:T], in_=pgb[:], func=AF.Sigmoid), r=[pgbn], w=[gbn])
                    pya, pyan = bank()
                    mm_fm(pya, pyan, sA, sAn, cbl, lambda kb: yap[:, kb * T:(kb + 1) * T], ["yap"])
                    S.op("dve", lambda e: e.tensor_tensor(out=ma[:, 0:T], in0=pya[:], in1=ga[:, 0:T], op=ALU.mult), r=[pyan, gan], w=[man])
                    tap_some(4)
                    return (cbl, ma, man, gb, gbn)

                def back(cbl, ma, man, gb, gbn):
                    ob = q4 * 4 + cbl
                    pyb, pybn = bank()
                    mm_fm(pyb, pybn, sB, sBn, cbl, lambda kb: cbo[:, kb * T:(kb + 1) * T], ["cbo"])
                    S.op("dve", lambda e: e.tensor_tensor(out=gb[:, 0:T], in0=pyb[:], in1=gb[:, 0:T], op=ALU.mult), r=[pybn, gbn], w=[gbn])
                    tap_some(4)
                    S.op("pool", lambda e: e.tensor_tensor(out=merged[:, ob * T:(ob + 1) * T], in0=ma[:, 0:T], in1=gb[:, 0:T], op=ALU.add),
                         r=[man, gbn], w=["n_t%d" % (ob // 2)])

                for cbl in range(4):
                    pend.append(front(cbl))
                    if len(pend) > 1:
                        back(*pend.pop(0))
                while pend:
                    back(*pend.pop(0))

        def p2_wo(q, i, k):
            merged = n_t
            xt, xn = xinfo(q, i, k)
            for half in range(2):
                so, son = load_w(wb_o, "wb_o", 0, 8, T * half)
                for j in range(4):
                    ph, phn = bank()
                    for kb in range(8):
                        S.op("pe", lambda e, kb=kb, j=j, ph=ph, so=so: e.matmul(ph[:], lhsT=merged[:, kb * T + j * 128: kb * T + (j + 1) * 128], rhs=so[:, kb, :],
                                                                              start=(kb == 0), stop=(kb == 7)), r=[son, "n_t%d" % (kb // 2)], w=[phn])
                    S.op("dve", lambda e, j=j, half=half, ph=ph: e.tensor_tensor(out=xt[:, j, half * T:(half + 1) * T], in0=ph[:], in1=xt[:, j, half * T:(half + 1) * T], op=ALU.add),
                         r=[phn, xn], w=[xn])
                    tap_some(2)
            dsth = hs.ap()[q, i * T:(i + 1) * T, :].rearrange("(j p) d -> p j d", p=128)
            S.dma("pool", "st_" + xn, lambda e: e.dma_start(out=dsth, in_=xt[:]), r=[xn], w=["h%d:%d" % (q, i)])

        def p2_norm2_b(q, i, k):
            norm_T(6, hgw, "hgw")
            dstn = n2s.ap()[q].rearrange("(cb p) t -> p cb t", p=128)[:, :, 1 + i * T:1 + (i + 1) * T]
            S.dma("pool", "st_nT", lambda e: e.dma_start(out=dstn, in_=hgw[:, :, 0:T]), r=["hgw%d" % kk_ for kk_ in range(8)], w=["n2%d:%d" % (q, i)])

        def run_pass2(tl):
            n = len(tl)
            for k, (q, i) in enumerate(tl):
                xt, xn = xinfo(q, i, k)
                nxt = k + 1 < n
                if k == 0:
                    p2_load(q, i, k)
                    rms_rstd(xt, xn, 0)
                    p2_conv(q, i, k)
                    norm_scale(xt, xn, 0)
                stt_ = p2_lnstats_mm()
                norm_T(0)
                p2_lnstats_chain(*stt_)
                p2_v()
                p2_umix()
                if nxt:
                    q2, i2 = tl[k + 1]
                    xt2, xn2 = xinfo(q2, i2, k + 1)
                    p2_load(q2, i2, k + 1)
                p2_outproj()
                if nxt:
                    rms_rstd(xt2, xn2, 0)
                    p2_conv(q2, i2, k + 1, range(0, 4))
                prep_some(2)
                p2_wo(q, i, k)
                rms_rstd(xt, xn, 4)
                norm_scale(xt, xn, 4)
                if nxt:
                    p2_conv(q2, i2, k + 1, range(4, 8))
                p2_norm2_b(q, i, k)
                if nxt:
                    norm_scale(xt2, xn2, 0)

        def pass3(q, i, hook=None):
            par = tile_id[0] % 2; tile_id[0] += 1
            s0 = i * T
            xt, xn = xbuf[par], "xbuf%d" % par
            n2w = hgw
            S.dma("sp", xn, lambda e: e.dma_start(out=xt[:], in_=load_rows(hs, q, s0, par)), r=["h%d:%d" % (q, i)], w=[xn])
            src = n2s.ap()[q].rearrange("(cb p) t -> p cb t", p=128)[:, :, s0:s0 + N2W]
            S.dma("sp", "hgw", lambda e: e.dma_start(out=n2w[:, :, 0:N2W], in_=src),
                  r=["n2z%d" % q] + ["n2%d:%d" % (q, k) for k in (i - 1, i, i + 1) if 0 <= k < NT], w=["hgw%d" % kk_ for kk_ in range(8)])
            arena_names = ["vhat", "yap", "cbo"]
            first_tile = not st.get("p3_started", False)
            st["p3_started"] = True
            for c4 in range(6):
                nfb = 4 if c4 < 5 else 2
                sg_, sgn = load_w(wb_up, "wb_up", 0, 8, c4 * T, nfb * 128)
                sv_, svn = load_w(wb_up, "wb_up", 0, 8, FFN + c4 * T, nfb * 128)
                for cbl in range(nfb):
                    fb = c4 * 4 + cbl
                    pgA, pgAn = bank()
                    mm_fm(pgA, pgAn, sg_, sgn, cbl, lambda kb: n2w[:, kb, 0:HN], ["hgw*"], n0=0, n1=HN)
                    pgB, pgBn = bank()
                    mm_fm(pgB, pgBn, sg_, sgn, cbl, lambda kb: n2w[:, kb, HN:N2W], ["hgw*"], n0=0, n1=HN)
                    pv, pvn = bank()
                    mm_fm(pv, pvn, sv_, svn, cbl, lambda kb: n2w[:, kb, 1:T + 1], ["hgw*"])
                    gs, gsn = tmp(); c1, c1n = tmp(); ge, gen = tmp()
                    S.op("act", lambda e, pgA=pgA, gs=gs: e.activation(out=gs[:, 0:HN], in_=pgA[:, 0:HN], func=AF.Copy), r=[pgAn], w=[gsn])
                    S.op("act", lambda e, pgB=pgB, gs=gs: e.activation(out=gs[:, HN:N2W], in_=pgB[:, 0:HN], func=AF.Copy), r=[pgBn], w=[gsn])
                    S.op("dve", lambda e, fb=fb, gs=gs, c1=c1: e.tensor_scalar(out=c1[:, 0:T], in0=gs[:, 1:T + 1], scalar1=FT[:, fb, 1:2], scalar2=FT[:, fb, 3:4],
                                                                            op0=ALU.mult, op1=ALU.add), r=[gsn, "FT"], w=[c1n])
                    S.op("dve", lambda e, fb=fb, gs=gs, c1=c1: e.scalar_tensor_tensor(out=c1[:, 0:T], in0=gs[:, 0:T], scalar=FT[:, fb, 0:1], in1=c1[:, 0:T],
                                                                                   op0=ALU.mult, op1=ALU.add), r=[gsn, c1n, "FT"], w=[c1n])
                    S.op("dve", lambda e, fb=fb, gs=gs, c1=c1: e.scalar_tensor_tensor(out=c1[:, 0:T], in0=gs[:, 2:T + 2], scalar=FT[:, fb, 2:3], in1=c1[:, 0:T],
                                                                                   op0=ALU.mult, op1=ALU.add), r=[gsn, c1n, "FT"], w=[c1n])
                    S.op("act", lambda e, c1=c1, ge=ge: e.activation(out=ge[:, 0:T], in_=c1[:, 0:T], func=AF.Gelu_apprx_tanh), r=[c1n], w=[gen])
                    S.op("dve", lambda e, fb=fb, ge=ge, pv=pv: e.tensor_tensor(out=actf[:, fb * T:(fb + 1) * T], in0=pv[:], in1=ge[:, 0:T], op=ALU.mult),
                         r=[pvn, gen], w=["act%d" % fb] + (arena_names if first_tile else []))
                    if hook and fb >= 3:
                        hook.pop(0)()
            for half in range(2):
                phs = [bank() for _ in range(4)]
                for (kb0, nkb) in ((0, 8), (8, 8), (16, 6)):
                    sd, sdn = load_w(wb_dn, "wb_dn", kb0, nkb, T * half)
                    for j in range(4):
                        ph, phn = phs[j]
                        for kk in range(nkb):
                            fb = kb0 + kk
                            S.op("pe", lambda e, fb=fb, kk=kk, j=j, ph=ph, sd=sd: e.matmul(ph[:], lhsT=actf[:, fb * T + j * 128: fb * T + (j + 1) * 128], rhs=sd[:, kk, :],
                                                                                         start=(fb == 0), stop=(fb == NFB - 1)), r=[sdn, "act%d" % fb], w=[phn])
                for j in range(4):
                    ph, phn = phs[j]
                    S.op("dve", lambda e, j=j, half=half, ph=ph: e.tensor_tensor(out=xt[:, j, half * T:(half + 1) * T], in0=ph[:], in1=xt[:, j, half * T:(half + 1) * T], op=ALU.add),
                         r=[phn, xn], w=[xn])
            def final_steps():
                steps = []

                def sq(j, last):
                    def f():
                        S.op("act", lambda e: e.activation(out=junk[:], in_=xt[:, j, :], func=AF.Square, accum_out=ss[:, j:j + 1]),
                             r=[xn], w=["junk", "ss0"])
                        if last:
                            S.op("dve", lambda e: e.tensor_scalar(out=ms[:, 0:4], in0=ss[:, 0:4], scalar1=1.0 / D, scalar2=EPS, op0=ALU.mult, op1=ALU.add),
                                 r=["ss0"], w=["ms0"])
                            S.op("pool", lambda e: e.tensor_tensor(out=rstd[:, 0:4], in0=ms[:, 0:4], in1=mhalf[:, 0:4], op=ALU.pow),
                                 r=["ms0", "mhalf"], w=["rstd0"])
                    return f

                def sc(j, last):
                    def f():
                        extra = [] if st.get("fin_started") else (["bufB"] + ["cvf%d" % c_ for c_ in range(8)])
                        S.op("act", lambda e: e.activation(out=bufB[:, j * 1024:(j + 1) * 1024], in_=xt[:, j, :], func=AF.Copy, scale=rstd[:, j:j + 1]),
                             r=[xn, "rstd0"], w=["fin%d" % j] + extra)
                        S.op("pool", lambda e: e.tensor_tensor(out=bufB[:, j * 1024:(j + 1) * 1024], in0=bufB[:, j * 1024:(j + 1) * 1024], in1=fg_rep, op=ALU.mult),
                             r=["fin%d" % j, "Bh"], w=["fin%d" % j])
                        if last:
                            st["fin_started"] = True
                            dsto = out_d.ap()[q, s0:s0 + T, :].rearrange("(j p) d -> p j d", p=128)
                            S.dma("pool", "st_bufB", lambda e: e.dma_start(out=dsto, in_=bufB[:].rearrange("p (j d) -> p j d", j=4)),
                                  r=["fin%d" % jj for jj in range(4)], w=["out%d:%d" % (q, i)])
                    return f
                for j in range(4):
                    steps.append(sq(j, j == 3))
                for j in range(4):
                    steps.append(sc(j, j == 3))
                return steps
            return final_steps()

        if tiles is None:
            tiles = [(q, i) for q in range(NSEQ) for i in range(NT)]
            tl = [tiles, tiles, tiles]
        else:
            tl = [[(0, i) for i in range(min(NT, tiles + 2 - k))] for k in range(3)]
        if npass >= 1:
            run_pass1(tl[0])
        else:
            prep_some(1000)
            run_deferred(1000)
        if npass >= 2:
            run_pass2(tl[1])
        prep_some(1000)
        if npass >= 3:
            S.dma("sp", "setup1", lambda e: e.dma_start(out=fg_rep[:], in_=bass.AP(final_g, 0, [[0, 128], [1, D]])), w=["Bh"])
            st["nring"] = NW + 2
            st["nbank"] = 8
            pend3 = None
            for (q, i) in tl[2]:
                pend3 = pass3(q, i, pend3)
            while pend3:
                pend3.pop(0)()

        keys = ["pe", "act", "dve", "pool"] + ["dma:" + k for k in S.dma_keys()]
        sems = {k: es.enter_context(nc.semaphore(k.replace(":", "_"))) for k in keys}
        cnt = S.emit(sems)
        for k, v in cnt.items():
            if k.startswith("dma:"):
                nc.sync.wait_ge(sems[k], v)
        print("ops:", len(S.ops), "sem counts:", {k: v for k, v in cnt.items() if not k.startswith("dma:")})
    return nc


_NC_CACHE = {}


def kernel(x, norm1_g, w_in, sg_ln_g, sg_ln_b, sg_w, sg_b, w_a_out, cv_dw_w, cv_dw_b,
           cv_ln_g, cv_ln_b, w_b_out, w_o, norm2_g, w_up, ffn_dw_w, ffn_dw_b, w_down, final_g):
    n = 8
    f = lambda a: np.ascontiguousarray(np.asarray(a, dtype=np.float32))
    x = f(x)
    shared = dict(norm1_g=f(norm1_g), w_in=f(w_in), sg_ln_g=f(sg_ln_g), sg_ln_b=f(sg_ln_b), sg_w=f(sg_w), sg_b=f(sg_b),
                  w_a_out=f(w_a_out), cv_dw_w=f(cv_dw_w), cv_dw_b=f(cv_dw_b), cv_ln_g=f(cv_ln_g), cv_ln_b=f(cv_ln_b),
                  w_b_out=f(w_b_out), w_o=f(w_o), norm2_g=f(norm2_g), w_up=f(w_up), ffn_dw_w=f(ffn_dw_w),
                  ffn_dw_b=f(ffn_dw_b), w_down=f(w_down), final_g=f(final_g))
    if "nc" not in _NC_CACHE:
        _NC_CACHE["nc"] = build_nc()
    nc = _NC_CACHE["nc"]
    in_maps = [dict(shared, x=x[NSEQ * c:NSEQ * (c + 1)]) for c in range(n)]
    res = run_bass_kernel_spmd(nc, in_maps, core_ids=list(range(n)))
    return np.concatenate([np.asarray(r["out"]) for r in res.results], axis=0).astype(np.float32)
```

```python
import numpy as np
from contextlib import ExitStack
import concourse.bass as bass
import concourse.mybir as mybir
from concourse.bass_utils import run_bass_kernel_spmd

F32 = mybir.dt.float32
BF16 = mybir.dt.bfloat16
AF = mybir.ActivationFunctionType
ALU = mybir.AluOpType

D = 1024
SEQ = 4096
NSEQ = 2
T = 512
NT = SEQ // T
FFN = 2816
NFB = FFN // 128
INC = 6144
EPS = 1e-6
HGW = T + 30
N2W = T + 2
HN = N2W // 2


class Sched:
    def __init__(self, nc):
        self.nc = nc
        self.ops = []
        self.eng = {"pe": nc.tensor, "act": nc.scalar, "dve": nc.vector,
                    "pool": nc.gpsimd, "sp": nc.sync}

    def op(self, eng, fn, r=(), w=()):
        self.ops.append((eng, fn, tuple(r), tuple(w)))

    def dma(self, queue, key, fn, r=(), w=()):
        self.ops.append((("dma", queue, key), fn, tuple(r), tuple(w)))

    def dma_keys(self):
        return sorted({e[2] for e, _, _, _ in self.ops if isinstance(e, tuple)})

    def emit(self, sems):
        ops = self.ops
        n = len(ops)
        last_w, readers, last_key = {}, {}, {}
        deps = [None] * n
        needed = [False] * n
        for i, (eng, fn, r, w) in enumerate(ops):
            d = set()
            for x in r:
                if x in last_w:
                    d.add(last_w[x])
            for x in w:
                if x in last_w:
                    d.add(last_w[x])
                d.update(readers.get(x, ()))
            if isinstance(eng, tuple):
                if eng[2] in last_key:
                    d.add(last_key[eng[2]])
                last_key[eng[2]] = i
            d.discard(i)
            if eng == "pe":
                d = {j for j in d if ops[j][0] != "pe"}
            deps[i] = d
            for j in d:
                needed[j] = True
            for x in r:
                readers.setdefault(x, []).append(i)
            for x in w:
                last_w[x] = i
                readers[x] = []
        cnt = {}
        sig = [None] * n
        for i, (eng, fn, r, w) in enumerate(ops):
            if isinstance(eng, tuple):
                key = "dma:" + eng[2]
                cnt[key] = cnt.get(key, 0) + 16
                sig[i] = (key, cnt[key])
            elif needed[i]:
                cnt[eng] = cnt.get(eng, 0) + 1
                sig[i] = (eng, cnt[eng])
        waited = {}
        for i, (eng, fn, r, w) in enumerate(ops):
            issuer = eng[1] if isinstance(eng, tuple) else eng
            e = self.eng[issuer]
            wd = waited.setdefault(issuer, {})
            need = {}
            for j in deps[i]:
                s, v = sig[j]
                if v > need.get(s, 0):
                    need[s] = v
            for s, v in need.items():
                if wd.get(s, 0) < v:
                    e.wait_ge(sems[s], v)
                    wd[s] = v
            inst = fn(e)
            if sig[i] is not None:
                inst.then_inc(sems[sig[i][0]], 16 if isinstance(eng, tuple) else 1)
        return cnt


def build_nc(debug=False, npass=3, tiles=None):
    nc = bass.Bass("TRN2", target_bir_lowering=False)
    dt_in = lambda name, shape: nc.dram_tensor(name, shape, F32, kind="ExternalInput")
    x_d = dt_in("x", [NSEQ, SEQ, D])
    norm1_g = dt_in("norm1_g", [1, D]); w_in = dt_in("w_in", [1, D, INC])
    sg_ln_g = dt_in("sg_ln_g", [1, D]); sg_ln_b = dt_in("sg_ln_b", [1, D])
    sg_w = dt_in("sg_w", [1, 8, 128, 128]); sg_b = dt_in("sg_b", [1, 8, 128])
    w_a_out = dt_in("w_a_out", [1, D, D])
    cv_dw_w = dt_in("cv_dw_w", [1, 31, D]); cv_dw_b = dt_in("cv_dw_b", [1, D])
    cv_ln_g = dt_in("cv_ln_g", [1, D]); cv_ln_b = dt_in("cv_ln_b", [1, D])
    w_b_out = dt_in("w_b_out", [1, D, D]); w_o = dt_in("w_o", [1, D, D])
    norm2_g = dt_in("norm2_g", [1, D]); w_up = dt_in("w_up", [1, D, 2 * FFN])
    ffn_dw_w = dt_in("ffn_dw_w", [1, 3, FFN]); ffn_dw_b = dt_in("ffn_dw_b", [1, FFN])
    w_down = dt_in("w_down", [1, FFN, D]); final_g = dt_in("final_g", [D])
    out_d = nc.dram_tensor("out", [NSEQ, SEQ, D], F32, kind="ExternalOutput")
    skind = "ExternalOutput" if debug else "Internal"
    hgs = nc.dram_tensor("hgs", [NSEQ, D, SEQ + 30], BF16, kind=skind)
    hs = nc.dram_tensor("hs", [NSEQ, SEQ, D], F32, kind=skind)
    n2s = nc.dram_tensor("n2s", [NSEQ, D, SEQ + 2], BF16, kind=skind)
    wb_in = nc.dram_tensor("wb_in", [D, INC], BF16, kind="Internal")
    wb_a = nc.dram_tensor("wb_a", [D, D], BF16, kind="Internal")
    wb_b = nc.dram_tensor("wb_b", [D, D], BF16, kind="Internal")
    wb_o = nc.dram_tensor("wb_o", [D, D], BF16, kind="Internal")
    wb_up = nc.dram_tensor("wb_up", [D, 2 * FFN], BF16, kind="Internal")
    wb_dn = nc.dram_tensor("wb_dn", [FFN, D], BF16, kind="Internal")
    dgs = nc.dram_tensor("dgs", [8, 128, 31, 128], BF16, kind="Internal")

    import os
    SKIP = set(os.environ.get('KSKIP', '').split(','))
    NW = 5
    NTMP = 8
    with ExitStack() as es:
        def sb(name, shape, dt=F32):
            return es.enter_context(nc.sbuf_tensor(name, shape, dt))

        xbuf = [sb("xbuf%d" % i, [128, 4, D]) for i in range(2)]
        n_t = sb("n_t", [128, 4096], BF16)
        nT = sb("nT", [128, 8, T], BF16)
        bufB = sb("bufB", [128, 4096])
        bufC = sb("bufC", [128, 4096])
        arena = sb("arena", [128, 3, 4096], BF16)
        hgw = sb("hgw", [128, 8, HGW + 2], BF16)
        tmps = [sb("tmp%d" % i, [128, T + 2]) for i in range(NTMP)]
        dg = [sb("dg%d" % i, [128, 31, 128], BF16) for i in range(2)]
        lnt = [sb("lnt%d" % i, [128, T]) for i in range(2)]
        vbuf = [sb("vbuf%d" % i, [128, D]) for i in range(2)]
        wbuf = [sb("wbuf%d" % i, [128, 8, T], BF16) for i in range(NW)]
        ident_b = sb("ident_b", [128, 128], BF16); ident_f = sb("ident_f", [128, 128])
        ones_b = sb("ones_b", [128, 128], BF16); ones1_b = sb("ones1_b", [128, 128], BF16)
        VT = sb("VT", [128, 8, 38]); FT = sb("FT", [128, NFB, 4])
        wsT_b = sb("wsT_b", [128, 8, 128], BF16); Bh = sb("Bh", [128, 8, 128])
        fg_rep = Bh[:].rearrange("p h i -> p (h i)")
        mhalf = sb("mhalf", [128, 8])
        junk = sb("junk", [128, D], BF16)
        ss = sb("ss", [128, 8]); ms = sb("ms", [128, 8]); rstd = sb("rstd", [128, 8])
        bst = sb("bst", [128, 4, 2, 6]); mv = sb("mv", [128, 4, 2]); vr = sb("vr", [128, 4]); vrs = sb("vrs", [128, 4])
        zer = sb("zer", [128, 8, 16], BF16)
        pbs = [es.enter_context(nc.psum_tensor("pb%d" % i, [128, T], F32)) for i in range(8)]
        ptrs = [pbs[6].bitcast(BF16), pbs[7].bitcast(BF16)]

        vhat = arena[:, 0, :]
        yap = arena[:, 1, :]
        cbo = arena[:, 2, :]
        actf = arena[:].rearrange("p a b -> p (a b)")
        convb = bufC.bitcast(BF16)

        S = Sched(nc)
        st = {"bank": 0, "tmp": 0, "w": 0, "prep": 0}

        def bank():
            nb = st.get("nbank", 6)
            b = st["bank"] % nb; st["bank"] = (b + 1) % nb
            return pbs[b], ("pb%d" % b) if b < 6 else ("ptr%d" % (b - 6))

        def tmp():
            i = st["tmp"]; st["tmp"] = (i + 1) % NTMP
            return tmps[i], "tmp%d" % i

        prep_q = []

        def prep(src_ap, dst, name, rows, c0=0, c1=None, cs=0):
            for kb in range(rows // 128):
                prep_q.append((src_ap, dst, name, kb, c0, c1, cs))

        def prep_some(n):
            for _ in range(min(n, len(prep_q))):
                src_ap, dst, name, kb, c0, c1, cs = prep_q.pop(0)
                k = st["prep"]; st["prep"] = (k + 1) % 8
                c1 = dst.shape[1] if c1 is None else c1
                S.dma("pool", "prep%d" % k,
                      lambda e, kb=kb, dst=dst, src_ap=src_ap, c0=c0, c1=c1: e.dma_start(out=dst.ap()[kb * 128:(kb + 1) * 128, c0:c1],
                                                                                       in_=src_ap[kb * 128:(kb + 1) * 128, c0:c1]),
                      r=list(st.get("prep_after", [])), w=["%s:%d:%d" % (name, kb, cs)])
        prep(w_in.ap()[0], wb_in, "wb_in", D, 2048, 4096, 1)
        if 'prep' in SKIP:
            prep_q.clear()
        prep(w_in.ap()[0], wb_in, "wb_in", D, 0, 2048, 0)
        prep(w_in.ap()[0], wb_in, "wb_in", D, 4096, 6144, 2)
        prep(w_a_out.ap()[0], wb_a, "wb_a", D)
        prep(w_b_out.ap()[0], wb_b, "wb_b", D)
        prep(w_o.ap()[0], wb_o, "wb_o", D)
        prep(w_up.ap()[0], wb_up, "wb_up", D)
        prep(w_down.ap()[0], wb_dn, "wb_dn", FFN)

        wbuf_x = [bufC.bitcast(BF16)[:, kk * 4096:(kk + 1) * 4096].rearrange("p (a b) -> p a b", a=8) for kk in range(2)]

        S.op("pool", lambda e: e.memset(ident_f[:], 0.0), w=["ident_f"])
        S.op("pool", lambda e: e.affine_select(out=ident_f[:], in_=ident_f[:], pattern=[[-1, 128]],
                                               compare_op=ALU.not_equal, fill=1.0, base=0, channel_multiplier=1),
             r=["ident_f"], w=["ident_f"])
        S.op("dve", lambda e: e.tensor_copy(out=ident_b[:], in_=ident_f[:]), r=["ident_f"], w=["ident_b"])
        S.op("pool", lambda e: e.memset(ones_b[:], 1.0 / D), w=["ones_b"])
        S.op("pool", lambda e: e.memset(ones1_b[:], 1.0), w=["ones1_b"])
        S.op("pool", lambda e: e.memset(mhalf[:], -0.5), w=["mhalf"])
        S.op("pool", lambda e: e.memset(zer[:], 0.0), w=["zer"])
        vrows = bufB[0:38, 0:D]
        frows = bufB[0:4, D:D + FFN]
        vec_list = [norm1_g.ap()[0], sg_ln_g.ap()[0], sg_ln_b.ap()[0], cv_dw_b.ap()[0], cv_ln_g.ap()[0],
                    cv_ln_b.ap()[0], norm2_g.ap()[0]]
        for i, v in enumerate(vec_list):
            S.dma("sp", "setup%d" % (i % 4), lambda e, i=i, v=v: e.dma_start(out=bufB[i:i + 1, 0:D], in_=v.rearrange("(o d) -> o d", o=1)),
                  w=["bufB"])
        S.dma("sp", "setup0", lambda e: e.dma_start(out=bufB[7:38, 0:D], in_=cv_dw_w.ap()[0]), w=["bufB"])
        S.dma("sp", "setup1", lambda e: e.dma_start(out=bufB[0:3, D:D + FFN], in_=ffn_dw_w.ap()[0]), w=["bufB"])
        S.dma("sp", "setup2", lambda e: e.dma_start(out=bufB[3:4, D:D + FFN], in_=ffn_dw_b.ap()[0].rearrange("(o d) -> o d", o=1)), w=["bufB"])
        st["prep_after"] = ["bufB"]
        prep_some(8)
        st["prep_after"] = []
        for cb in range(0 if 'vec' in SKIP else 8):
            pb, pn = bank()
            S.op("pe", lambda e, cb=cb, pb=pb: e.transpose(out=pb[:, 0:38], in_=vrows[:, cb * 128:(cb + 1) * 128], identity=ident_f[0:38, 0:38]),
                 r=["bufB", "ident_f"], w=[pn])
            S.op("dve", lambda e, cb=cb, pb=pb: e.tensor_copy(out=VT[:, cb, :], in_=pb[:, 0:38]), r=[pn], w=["VT"])
        deferred = []

        def _d_ft():
            for fb in range(0 if 'vec' in SKIP else NFB):
                pb, pn = bank()
                S.op("pe", lambda e, fb=fb, pb=pb: e.transpose(out=pb[:, 0:4], in_=frows[:, fb * 128:(fb + 1) * 128], identity=ident_f[0:4, 0:4]),
                     r=["bufB", "ident_f"], w=[pn])
                S.op("dve", lambda e, fb=fb, pb=pb: e.tensor_copy(out=FT[:, fb, :], in_=pb[:, 0:4]), r=[pn], w=["FT"])
        deferred.append(_d_ft)

        swl = bufC[:, 0:1024].rearrange("p (h j) -> p h j", h=8)
        sgb_rep = bufC[:, 2048:3072]
        S.dma("sp", "setup3", lambda e: e.dma_start(out=swl, in_=sg_w.ap()[0].rearrange("h i j -> i h j")), w=["bufC"])
        S.dma("sp", "setup0", lambda e: e.dma_start(out=sgb_rep, in_=bass.AP(sg_b, 0, [[0, 128], [1, 1024]])), w=["bufC"])

        def _d_sg():
            swb = arena[:, 0, 0:1024].rearrange("p (h j) -> p h j", h=8)
            S.op("dve", lambda e: e.tensor_copy(out=swb, in_=swl), r=["bufC"], w=["vhat"])
            for g4 in range(0 if 'sgt' in SKIP else 2):
                pn = "ptr%d" % g4
                for hh in range(4):
                    hd = g4 * 4 + hh
                    S.op("pe", lambda e, hd=hd, hh=hh, g4=g4: e.transpose(out=ptrs[g4][:, hh * 128:(hh + 1) * 128], in_=swb[:, hd, :], identity=ident_b[:]),
                         r=["vhat", "ident_b"], w=[pn])
                S.op("act", lambda e, g4=g4: e.activation(out=wsT_b[:, g4 * 4:(g4 + 1) * 4, :].rearrange("p h i -> p (h i)"), in_=ptrs[g4][:, 0:512], func=AF.Copy),
                     r=[pn], w=["wsT_b"])
            for g4 in range(0 if 'rows' in SKIP else 2):
                pb, pn = bank()
                S.op("pe", lambda e, g4=g4, pb=pb: e.matmul(pb[:], lhsT=ones1_b[:], rhs=wsT_b[:, g4 * 4:(g4 + 1) * 4, :].rearrange("p h i -> p (h i)"), start=True, stop=True),
                     r=["wsT_b", "ones1_b"], w=[pn])
                for hh in range(4):
                    hd = g4 * 4 + hh
                    S.op("dve", lambda e, hd=hd, hh=hh, pb=pb: e.scalar_tensor_tensor(
                        out=Bh[:, hd, :], in0=pb[:, hh * 128:(hh + 1) * 128], scalar=VT[:, hd, 2:3],
                        in1=sgb_rep[:, hd * 128:(hd + 1) * 128], op0=ALU.mult, op1=ALU.add), r=[pn, "VT", "bufC"], w=["Bh"])

        def _d_diag(cb):
            dgi = cb % 2
            S.op("pool", lambda e: e.tensor_tensor(
                out=dg[dgi][:], in0=bass.AP(ident_f, 0, [[128, 128], [0, 31], [1, 128]]),
                in1=bass.AP(VT, cb * 38 + 7, [[8 * 38, 128], [1, 31], [0, 128]]), op=ALU.mult), r=["ident_f", "VT"], w=["dg%d" % dgi])
            S.dma("pool", "dgst%d" % dgi, lambda e: e.dma_start(out=dgs.ap()[cb], in_=dg[dgi][:]), r=["dg%d" % dgi], w=["dgs%d" % cb])
        def _d_zero():
            for q in range(0 if 'zero' in SKIP else NSEQ):
                hq = hgs.ap()[q].rearrange("(cb p) t -> p cb t", p=128)
                nq = n2s.ap()[q].rearrange("(cb p) t -> p cb t", p=128)
                S.dma("pool", "z0", lambda e, hq=hq: e.dma_start(out=hq[:, :, 0:15], in_=zer[:, :, 0:15]), r=["zer"], w=["hgz%d" % q])
                S.dma("pool", "z1", lambda e, hq=hq: e.dma_start(out=hq[:, :, SEQ + 15:SEQ + 30], in_=zer[:, :, 0:15]), r=["zer"], w=["hgz%d" % q])
                S.dma("pool", "z2", lambda e, nq=nq: e.dma_start(out=nq[:, :, 0:1], in_=zer[:, :, 0:1], allow_slow_non_contiguous=True), r=["zer"], w=["n2z%d" % q])
                S.dma("pool", "z3", lambda e, nq=nq: e.dma_start(out=nq[:, :, SEQ + 1:SEQ + 2], in_=zer[:, :, 0:1], allow_slow_non_contiguous=True), r=["zer"], w=["n2z%d" % q])

        deferred.append(_d_zero)
        for cb in range(8):
            deferred.append(lambda cb=cb: _d_diag(cb))

        deferred.append(_d_sg)

        def run_deferred(n):
            for _ in range(min(n, len(deferred))):
                deferred.pop(0)()

        def load_w(wt, name, kb0, nkb, c0, ncols=T):
            nring = st.get("nring", NW)
            s = st["w"] % nring; st["w"] = (s + 1) % nring
            if s >= NW:
                wb_ = wbuf_x[s - NW]
                src = wt.ap().rearrange("(kb p) c -> p kb c", p=128)[:, kb0:kb0 + nkb, c0:c0 + ncols]
                S.dma("sp", "w%d" % s, lambda e: e.dma_start(out=wb_[:, 0:nkb, 0:ncols], in_=src),
                      r=["%s:%d:%d" % (name, kb, (c0 // 2048) if name == "wb_in" else 0) for kb in range(kb0, kb0 + nkb)], w=["w%d" % s, "bufC"])
                return wb_, "w%d" % s
            src = wt.ap().rearrange("(kb p) c -> p kb c", p=128)[:, kb0:kb0 + nkb, c0:c0 + ncols]
            S.dma("sp", "w%d" % s, lambda e: e.dma_start(out=wbuf[s][:, 0:nkb, 0:ncols], in_=src),
                  r=["%s:%d:%d" % (name, kb, (c0 // 2048) if name == "wb_in" else 0) for kb in range(kb0, kb0 + nkb)], w=["w%d" % s])
            return wbuf[s], "w%d" % s

        def load_rows(dram3, q, s0, par):
            xt = xbuf[par]
            src = dram3.ap()[q, s0:s0 + T, :].rearrange("(j p) d -> p j d", p=128)
            return src

        def rms_rstd(xt, xn, col0):
            for j in range(4):
                S.op("act", lambda e, j=j: e.activation(out=junk[:], in_=xt[:, j, :], func=AF.Square, accum_out=ss[:, col0 + j:col0 + j + 1]),
                     r=[xn], w=["junk", "ss%d" % col0])
            S.op("dve", lambda e: e.tensor_scalar(out=ms[:, col0:col0 + 4], in0=ss[:, col0:col0 + 4], scalar1=1.0 / D, scalar2=EPS, op0=ALU.mult, op1=ALU.add),
                 r=["ss%d" % col0], w=["ms%d" % col0])
            S.op("pool", lambda e: e.tensor_tensor(out=rstd[:, col0:col0 + 4], in0=ms[:, col0:col0 + 4], in1=mhalf[:, 0:4], op=ALU.pow),
                 r=["ms%d" % col0, "mhalf"], w=["rstd%d" % col0])

        def norm_scale(xt, xn, col0):
            for j in range(4):
                if j % 2 == 0:
                    S.op("dve", lambda e, j=j: e.tensor_scalar(out=n_t[:, j * 1024:(j + 1) * 1024], in0=xt[:, j, :], scalar1=rstd[:, col0 + j:col0 + j + 1],
                                                               scalar2=None, op0=ALU.mult),
                         r=[xn, "rstd%d" % col0], w=["n_t%d" % j])
                else:
                    S.op("act", lambda e, j=j: e.activation(out=n_t[:, j * 1024:(j + 1) * 1024], in_=xt[:, j, :], func=AF.Copy, scale=rstd[:, col0 + j:col0 + j + 1]),
                         r=[xn, "rstd%d" % col0], w=["n_t%d" % j])

        def norm_T(gidx, dst=None, dstn="nT"):
            dst = nT if dst is None else dst
            for kb in range(8):
                half = kb % 2
                pn = "ptr%d" % half
                for j in range(4):
                    S.op("pe", lambda e, kb=kb, j=j, half=half: e.transpose(
                        out=ptrs[half][:, j * 128:(j + 1) * 128],
                        in_=n_t[:, j * 1024 + kb * 128: j * 1024 + (kb + 1) * 128], identity=ident_b[:]),
                        r=["n_t%d" % j, "ident_b"], w=[pn])
                if kb % 2 == 0:
                    S.op("act", lambda e, kb=kb, half=half: e.activation(out=dst[:, kb, 0:T], in_=ptrs[half][:, 0:512], func=AF.Copy,
                                                                         scale=VT[:, kb, gidx:gidx + 1]),
                         r=[pn, "VT"], w=[dstn + str(kb)])
                else:
                    S.op("dve", lambda e, kb=kb, half=half: e.tensor_scalar(out=dst[:, kb, 0:T], in0=ptrs[half][:, 0:512],
                                                                            scalar1=VT[:, kb, gidx:gidx + 1], scalar2=None, op0=ALU.mult),
                         r=[pn, "VT"], w=[dstn + str(kb)])

        def mm_fm(pb, pn, wslot, wn, cbl, rhs_fn, rhs_names, nkb=8, n0=0, n1=T, rhs_shift=0):
            for kb in range(nkb):
                S.op("pe", lambda e, kb=kb: e.matmul(pb[:, n0:n1], lhsT=wslot[:, kb, cbl * 128:(cbl + 1) * 128], rhs=rhs_fn(kb),
                                                     start=(kb == 0), stop=(kb == nkb - 1)),
                     r=[wn] + [(x[:-1] + str(kb)) if x.endswith("*") else x for x in rhs_names], w=[pn])

        tile_id = [0]

        def xinfo(q, i, k):
            par = k % 2
            return xbuf[par], "xbuf%d" % par

        def p1_front(q, i, k):
            xt, xn = xinfo(q, i, k)
            S.dma("sp", xn, lambda e: e.dma_start(out=xt[:], in_=load_rows(x_d, q, i * T, 0)), w=[xn])
            rms_rstd(xt, xn, 0)
            norm_scale(xt, xn, 0)

        def p1_main(q, i, k):
            s0 = i * T
            hg_st, hgn = (yap, "yap") if k % 2 == 0 else (cbo, "cbo")
            for q4 in range(2):
                sa, san = load_w(wb_in, "wb_in", 0, 8, 2048 + T * q4)
                sg, sgn = load_w(wb_in, "wb_in", 0, 8, 3072 + T * q4)
                for cbl in range(4):
                    cb = q4 * 4 + cbl
                    pa, pan = bank()
                    mm_fm(pa, pan, sa, san, cbl, lambda kb: nT[:, kb, :], ["nT*"])
                    pg, pgn = bank()
                    mm_fm(pg, pgn, sg, sgn, cbl, lambda kb: nT[:, kb, :], ["nT*"])
                    t1, t1n = tmp()
                    S.op("act", lambda e, pg=pg, t1=t1: e.activation(out=t1[:, 0:T], in_=pg[:], func=AF.Sigmoid), r=[pgn], w=[t1n])
                    S.op("dve", lambda e, pa=pa, t1=t1, cb=cb: e.tensor_tensor(out=hg_st[:, cb * T:(cb + 1) * T], in0=pa[:], in1=t1[:, 0:T], op=ALU.mult),
                         r=[pan, t1n], w=[hgn])
            dst = hgs.ap()[q].rearrange("(cb p) t -> p cb t", p=128)[:, :, 15 + s0:15 + s0 + T]
            S.dma("pool", "st_" + hgn, lambda e: e.dma_start(out=dst, in_=hg_st.rearrange("p (cb t) -> p cb t", cb=8)), r=[hgn], w=["hg%d:%d" % (q, i)])

        def run_pass1(tl):
            n = len(tl)
            for k, (q, i) in enumerate(tl):
                if k == 0:
                    p1_front(q, i, k)
                    norm_T(0)
                if k + 1 < n:
                    p1_front(tl[k + 1][0], tl[k + 1][1], k + 1)
                prep_some(3)
                p1_main(q, i, k)
                run_deferred(2 if k == 0 else 1)
                if k + 1 < n:
                    norm_T(0)
            prep_some(max(0, len(prep_q) - 30))
            run_deferred(1000)

        def p2_load(q, i, k):
            xt, xn = xinfo(q, i, k)
            s0 = i * T
            S.dma("sp", xn, lambda e: e.dma_start(out=xt[:], in_=load_rows(x_d, q, s0, 0)), w=[xn])
            src = hgs.ap()[q].rearrange("(cb p) t -> p cb t", p=128)[:, :, s0:s0 + HGW]
            S.dma("sp", "hgw", lambda e: e.dma_start(out=hgw[:, :, 0:HGW], in_=src),
                  r=["hgz%d" % q] + ["hg%d:%d" % (q, kk) for kk in (i - 1, i, i + 1) if 0 <= kk < NT], w=["hgw%d" % kk_ for kk_ in range(8)])
            p2_taps_enqueue()

        KD = 10

        tapq = []

        def p2_taps_enqueue():
            first = not st.get("cvf_started", False)
            st["cvf_started"] = True
            k0 = 31 - (1 if first else KD)
            st["k0"] = k0
            for cb in range(8):
                dst = bufB[:, cb * T:(cb + 1) * T]
                wn = ["cvf%d" % cb] + (["bufB"] if first else [])
                tapq.append((cb, lambda cb=cb, dst=dst, wn=wn: S.op("dve", lambda e: e.tensor_scalar(
                    out=dst, in0=hgw[:, cb, k0:k0 + T], scalar1=VT[:, cb, 7 + k0:8 + k0], scalar2=VT[:, cb, 3:4],
                    op0=ALU.mult, op1=ALU.add), r=["hgw%d" % cb, "VT"], w=wn)))
                for kk in range(k0 + 1, 31):
                    tapq.append((cb, lambda cb=cb, kk=kk, dst=dst: S.op("dve", lambda e: e.scalar_tensor_tensor(
                        out=dst, in0=hgw[:, cb, kk:kk + T], scalar=VT[:, cb, 7 + kk:8 + kk], in1=dst,
                        op0=ALU.mult, op1=ALU.add), r=["hgw%d" % cb, "VT", "cvf%d" % cb], w=["cvf%d" % cb])))

        def tap_some(n, upto=None):
            while tapq and n > 0 and (upto is None or tapq[0][0] <= upto):
                tapq.pop(0)[1]()
                n -= 1

        def dg_load(cb):
            dgi = cb % 2
            k0 = st["k0"]
            S.dma("sp", "dg%d" % dgi, lambda e: e.dma_start(out=dg[dgi][:, 0:k0, :], in_=dgs.ap()[cb, :, 0:k0, :]), r=["dgs%d" % cb], w=["dg%d" % dgi])

        def p2_conv(q, i, k, cbs=range(8)):
            k0 = st["k0"]
            tap_some(10 ** 6, upto=max(cbs))
            for cb in cbs:
                dgi = cb % 2
                dgn = "dg%d" % dgi
                dst = bufB[:, cb * T:(cb + 1) * T]
                dg_load(cb)
                pc, pcn = bank()
                for kk in range(k0):
                    S.op("pe", lambda e, cb=cb, kk=kk, pc=pc, dgi=dgi: e.matmul(pc[:], lhsT=dg[dgi][:, kk, :], rhs=hgw[:, cb, kk:kk + T], start=(kk == 0), stop=(kk == k0 - 1)),
                         r=[dgn, "hgw%d" % cb], w=[pcn])
                S.op("dve", lambda e, pc=pc, dst=dst: e.tensor_tensor(out=dst, in0=pc[:], in1=dst, op=ALU.add), r=[pcn, "cvf%d" % cb], w=["cvf%d" % cb])
                S.op("act", lambda e, cb=cb, dst=dst: e.activation(out=convb[:, 4096 + cb * T: 4096 + (cb + 1) * T], in_=dst, func=AF.Square),
                     r=["cvf%d" % cb], w=["bufC"])
                S.op("pool", lambda e, cb=cb, dst=dst: e.tensor_copy(out=convb[:, cb * T:(cb + 1) * T], in_=dst), r=["cvf%d" % cb], w=["bufC"])

        def p2_lnstats_mm():
            pmean, pmeann = bank()
            for cb in range(8):
                S.op("pe", lambda e, cb=cb: e.matmul(pmean[:], lhsT=ones_b[:], rhs=convb[:, cb * T:(cb + 1) * T], start=(cb == 0), stop=(cb == 7)),
                     r=["ones_b", "bufC"], w=[pmeann])
            pmsq, pmsqn = bank()
            for cb in range(8):
                S.op("pe", lambda e, cb=cb: e.matmul(pmsq[:], lhsT=ones_b[:], rhs=convb[:, 4096 + cb * T: 4096 + (cb + 1) * T], start=(cb == 0), stop=(cb == 7)),
                     r=["ones_b", "bufC"], w=[pmsqn])
            return pmean, pmeann, pmsq, pmsqn

        def p2_lnstats_chain(pmean, pmeann, pmsq, pmsqn):
            mean_sb, meann = lnt[0], "lnt0"; rsb, rsbn = lnt[1], "lnt1"
            m2, m2n = tmp()
            S.op("act", lambda e: e.activation(out=mean_sb[:, 0:T], in_=pmean[:], func=AF.Copy), r=[pmeann], w=[meann])
            S.op("dve", lambda e: e.tensor_tensor(out=m2[:, 0:T], in0=mean_sb[:, 0:T], in1=mean_sb[:, 0:T], op=ALU.mult), r=[meann], w=[m2n])
            S.op("dve", lambda e: e.scalar_tensor_tensor(out=m2[:, 0:T], in0=pmsq[:], scalar=EPS, in1=m2[:, 0:T], op0=ALU.add, op1=ALU.subtract),
                 r=[pmsqn, m2n], w=[m2n])
            S.op("act", lambda e: e.activation(out=m2[:, 0:T], in_=m2[:, 0:T], func=AF.Sqrt), r=[m2n], w=[m2n])
            S.op("dve", lambda e: e.reciprocal(out=rsb[:, 0:T], in_=m2[:, 0:T]), r=[m2n], w=[rsbn])

        def p2_lnnorm(cb):
            mean_sb, meann = lnt[0], "lnt0"; rsb, rsbn = lnt[1], "lnt1"
            d1, d1n = tmp()
            S.op("dve", lambda e: e.tensor_tensor(out=d1[:, 0:T], in0=bufB[:, cb * T:(cb + 1) * T], in1=mean_sb[:, 0:T], op=ALU.subtract),
                 r=["cvf%d" % cb, meann], w=[d1n])
            S.op("dve", lambda e: e.tensor_tensor(out=d1[:, 0:T], in0=d1[:, 0:T], in1=rsb[:, 0:T], op=ALU.mult), r=[d1n, rsbn], w=[d1n])
            S.op("act", lambda e: e.activation(out=cbo[:, cb * T:(cb + 1) * T], in_=d1[:, 0:T], func=AF.Silu,
                                               scale=VT[:, cb, 4:5], bias=VT[:, cb, 5:6]), r=[d1n, "VT"], w=["cbo"])

        def p2_v():
            sv0, sv0n = load_w(wb_in, "wb_in", 0, 8, 1024)
            sv1, sv1n = load_w(wb_in, "wb_in", 0, 8, 1024 + T)
            for j in range(4):
                vb, vbn = vbuf[j % 2], "vbuf%d" % (j % 2)
                for half, (sv, svn) in enumerate(((sv0, sv0n), (sv1, sv1n))):
                    pv, pvn = bank()
                    for kb in range(8):
                        S.op("pe", lambda e, kb=kb, j=j, pv=pv, sv=sv: e.matmul(pv[:], lhsT=nT[:, kb, j * 128:(j + 1) * 128], rhs=sv[:, kb, :],
                                                                              start=(kb == 0), stop=(kb == 7)), r=[svn, "nT%d" % kb], w=[pvn])
                    S.op("act", lambda e, pv=pv, vb=vb, half=half: e.activation(out=vb[:, half * T:(half + 1) * T], in_=pv[:], func=AF.Gelu_apprx_tanh),
                         r=[pvn], w=[vbn])
                    S.op("dve", lambda e, j=j, half=half, vb=vb: e.bn_stats(out=bst[:, j, half, :], in_=vb[:, half * T:(half + 1) * T]), r=[vbn], w=["bst%d" % j])
                S.op("dve", lambda e, j=j: e.bn_aggr(out=mv[:, j, :], in_=bst[:, j, :, :].rearrange("p a b -> p (a b)")), r=["bst%d" % j], w=["mv%d" % j])
                S.op("dve", lambda e, j=j: e.tensor_scalar(out=vr[:, j:j + 1], in0=mv[:, j, 1:2], scalar1=EPS, scalar2=None, op0=ALU.add), r=["mv%d" % j], w=["vr%d" % j])
                S.op("pool", lambda e, j=j: e.tensor_tensor(out=vrs[:, j:j + 1], in0=vr[:, j:j + 1], in1=mhalf[:, 0:1], op=ALU.pow), r=["vr%d" % j, "mhalf"], w=["vrs%d" % j])
                S.op("dve", lambda e, j=j, vb=vb: e.tensor_scalar(out=vhat[:, j * 1024:(j + 1) * 1024], in0=vb[:],
                                                                  scalar1=mv[:, j, 0:1], scalar2=vrs[:, j:j + 1], op0=ALU.subtract, op1=ALU.mult),
                     r=[vbn, "mv%d" % j, "vrs%d" % j], w=["vhat"])

        def p2_umix():
            pend = []

            def mix(hd, ut, utn):
                pm, pmn = bank()
                for j in range(4):
                    S.op("pe", lambda e, j=j: e.matmul(pm[:, j * 128:(j + 1) * 128], lhsT=vhat[:, j * 1024 + hd * 128: j * 1024 + (hd + 1) * 128],
                                                       rhs=wsT_b[:, hd, :], start=True, stop=True), r=["vhat", "wsT_b"], w=[pmn])
                t1, t1n = tmp()
                bh_bc = bass.AP(Bh, hd * 128, [[1024, 128], [0, 4], [1, 128]])
                S.op("dve", lambda e: e.scalar_tensor_tensor(
                    out=t1[:, 0:T].rearrange("p (a b) -> p a b", a=4), in0=pm[:].rearrange("p (a b) -> p a b", a=4),
                    scalar=VT[:, hd, 1:2], in1=bh_bc, op0=ALU.mult, op1=ALU.add), r=[pmn, "VT", "Bh"], w=[t1n])
                S.op("pool", lambda e: e.tensor_tensor(out=yap[:, hd * T:(hd + 1) * T], in0=t1[:, 0:T], in1=ut[:, 0:T], op=ALU.mult),
                     r=[t1n, utn], w=["yap"])

            for q4 in range(2):
                su, sun = load_w(wb_in, "wb_in", 0, 8, T * q4)
                for cbl in range(4):
                    hd = q4 * 4 + cbl
                    pu, pun = bank()
                    mm_fm(pu, pun, su, sun, cbl, lambda kb: nT[:, kb, :], ["nT*"])
                    ut, utn = tmp()
                    S.op("act", lambda e, pu=pu, ut=ut: e.activation(out=ut[:, 0:T], in_=pu[:], func=AF.Gelu_apprx_tanh), r=[pun], w=[utn])
                    pend.append((hd, ut, utn))
                    if len(pend) > 3:
                        mix(*pend.pop(0))
            while pend:
                mix(*pend.pop(0))
            for cb in range(8):
                p2_lnnorm(cb)

        def p2_outproj():
            merged = n_t
            for q4 in range(2):
                sGA, sGAn = load_w(wb_in, "wb_in", 0, 8, 4096 + T * q4)
                sGB, sGBn = load_w(wb_in, "wb_in", 0, 8, 5120 + T * q4)
                sA, sAn = load_w(wb_a, "wb_a", 0, 8, T * q4)
                sB, sBn = load_w(wb_b, "wb_b", 0, 8, T * q4)
                pend = []

                def front(cbl):
                    pga, pgan = bank()
                    mm_fm(pga, pgan, sGA, sGAn, cbl, lambda kb: nT[:, kb, :], ["nT*"])
                    ga, gan = tmp(); ma, man = tmp()
                    S.op("act", lambda e: e.activation(out=ga[:, 0:T], in_=pga[:], func=AF.Sigmoid), r=[pgan], w=[gan])
                    pgb, pgbn = bank()
                    mm_fm(pgb, pgbn, sGB, sGBn, cbl, lambda kb: nT[:, kb, :], ["nT*"])
                    gb, gbn = tmp()
                    S.op("act", lambda e: e.activation(out=gb[:, 0:T], in_=pgb[:], func=AF.Sigmoid), r=[pgbn], w=[gbn])
                    pya, pyan = bank()
                    mm_fm(pya, pyan, sA, sAn, cbl, lambda kb: yap[:, kb * T:(kb + 1) * T], ["yap"])
                    S.op("dve", lambda e: e.tensor_tensor(out=ma[:, 0:T], in0=pya[:], in1=ga[:, 0:T], op=ALU.mult), r=[pyan, gan], w=[man])
                    tap_some(4)
                    return (cbl, ma, man, gb, gbn)

                def back(cbl, ma, man, gb, gbn):
                    ob = q4 * 4 + cbl
                    pyb, pybn = bank()
                    mm_fm(pyb, pybn, sB, sBn, cbl, lambda kb: cbo[:, kb * T:(kb + 1) * T], ["cbo"])
                    S.op("dve", lambda e: e.tensor_tensor(out=gb[:, 0:T], in0=pyb[:], in1=gb[:, 0:T], op=ALU.mult), r=[pybn, gbn], w=[gbn])
                    tap_some(4)
                    S.op("pool", lambda e: e.tensor_tensor(out=merged[:, ob * T:(ob + 1) * T], in0=ma[:, 0:T], in1=gb[:, 0:T], op=ALU.add),
                         r=[man, gbn], w=["n_t%d" % (ob // 2)])

                for cbl in range(4):
                    pend.append(front(cbl))
                    if len(pend) > 1:
                        back(*pend.pop(0))
                while pend:
                    back(*pend.pop(0))

        def p2_wo(q, i, k):
            merged = n_t
            xt, xn = xinfo(q, i, k)
            for half in range(2):
                so, son = load_w(wb_o, "wb_o", 0, 8, T * half)
                for j in range(4):
                    ph, phn = bank()
                    for kb in range(8):
                        S.op("pe", lambda e, kb=kb, j=j, ph=ph, so=so: e.matmul(ph[:], lhsT=merged[:, kb * T + j * 128: kb * T + (j + 1) * 128], rhs=so[:, kb, :],
                                                                              start=(kb == 0), stop=(kb == 7)), r=[son, "n_t%d" % (kb // 2)], w=[phn])
                    S.op("dve", lambda e, j=j, half=half, ph=ph: e.tensor_tensor(out=xt[:, j, half * T:(half + 1) * T], in0=ph[:], in1=xt[:, j, half * T:(half + 1) * T], op=ALU.add),
                         r=[phn, xn], w=[xn])
                    tap_some(2)
            dsth = hs.ap()[q, i * T:(i + 1) * T, :].rearrange("(j p) d -> p j d", p=128)
            S.dma("pool", "st_" + xn, lambda e: e.dma_start(out=dsth, in_=xt[:]), r=[xn], w=["h%d:%d" % (q, i)])

        def p2_norm2_b(q, i, k):
            norm_T(6, hgw, "hgw")
            dstn = n2s.ap()[q].rearrange("(cb p) t -> p cb t", p=128)[:, :, 1 + i * T:1 + (i + 1) * T]
            S.dma("pool", "st_nT", lambda e: e.dma_start(out=dstn, in_=hgw[:, :, 0:T]), r=["hgw%d" % kk_ for kk_ in range(8)], w=["n2%d:%d" % (q, i)])

        def run_pass2(tl):
            n = len(tl)
            for k, (q, i) in enumerate(tl):
                xt, xn = xinfo(q, i, k)
                nxt = k + 1 < n
                if k == 0:
                    p2_load(q, i, k)
                    rms_rstd(xt, xn, 0)
                    p2_conv(q, i, k)
                    norm_scale(xt, xn, 0)
                stt_ = p2_lnstats_mm()
                norm_T(0)
                p2_lnstats_chain(*stt_)
                p2_v()
                p2_umix()
                if nxt:
                    q2, i2 = tl[k + 1]
                    xt2, xn2 = xinfo(q2, i2, k + 1)
                    p2_load(q2, i2, k + 1)
                p2_outproj()
                if nxt:
                    rms_rstd(xt2, xn2, 0)
                    p2_conv(q2, i2, k + 1, range(0, 4))
                prep_some(2)
                p2_wo(q, i, k)
                rms_rstd(xt, xn, 4)
                norm_scale(xt, xn, 4)
                if nxt:
                    p2_conv(q2, i2, k + 1, range(4, 8))
                p2_norm2_b(q, i, k)
                if nxt:
                    norm_scale(xt2, xn2, 0)

        def pass3(q, i, hook=None):
            par = tile_id[0] % 2; tile_id[0] += 1
            s0 = i * T
            xt, xn = xbuf[par], "xbuf%d" % par
            n2w = hgw
            S.dma("sp", xn, lambda e: e.dma_start(out=xt[:], in_=load_rows(hs, q, s0, par)), r=["h%d:%d" % (q, i)], w=[xn])
            src = n2s.ap()[q].rearrange("(cb p) t -> p cb t", p=128)[:, :, s0:s0 + N2W]
            S.dma("sp", "hgw", lambda e: e.dma_start(out=n2w[:, :, 0:N2W], in_=src),
                  r=["n2z%d" % q] + ["n2%d:%d" % (q, k) for k in (i - 1, i, i + 1) if 0 <= k < NT], w=["hgw%d" % kk_ for kk_ in range(8)])
            arena_names = ["vhat", "yap", "cbo"]
            first_tile = not st.get("p3_started", False)
            st["p3_started"] = True
            for c4 in range(6):
                nfb = 4 if c4 < 5 else 2
                sg_, sgn = load_w(wb_up, "wb_up", 0, 8, c4 * T, nfb * 128)
                sv_, svn = load_w(wb_up, "wb_up", 0, 8, FFN + c4 * T, nfb * 128)
                for cbl in range(nfb):
                    fb = c4 * 4 + cbl
                    pgA, pgAn = bank()
                    mm_fm(pgA, pgAn, sg_, sgn, cbl, lambda kb: n2w[:, kb, 0:HN], ["hgw*"], n0=0, n1=HN)
                    pgB, pgBn = bank()
                    mm_fm(pgB, pgBn, sg_, sgn, cbl, lambda kb: n2w[:, kb, HN:N2W], ["hgw*"], n0=0, n1=HN)
                    pv, pvn = bank()
                    mm_fm(pv, pvn, sv_, svn, cbl, lambda kb: n2w[:, kb, 1:T + 1], ["hgw*"])
                    gs, gsn = tmp(); c1, c1n = tmp(); ge, gen = tmp()
                    S.op("act", lambda e, pgA=pgA, gs=gs: e.activation(out=gs[:, 0:HN], in_=pgA[:, 0:HN], func=AF.Copy), r=[pgAn], w=[gsn])
                    S.op("act", lambda e, pgB=pgB, gs=gs: e.activation(out=gs[:, HN:N2W], in_=pgB[:, 0:HN], func=AF.Copy), r=[pgBn], w=[gsn])
                    S.op("dve", lambda e, fb=fb, gs=gs, c1=c1: e.tensor_scalar(out=c1[:, 0:T], in0=gs[:, 1:T + 1], scalar1=FT[:, fb, 1:2], scalar2=FT[:, fb, 3:4],
                                                                            op0=ALU.mult, op1=ALU.add), r=[gsn, "FT"], w=[c1n])
                    S.op("dve", lambda e, fb=fb, gs=gs, c1=c1: e.scalar_tensor_tensor(out=c1[:, 0:T], in0=gs[:, 0:T], scalar=FT[:, fb, 0:1], in1=c1[:, 0:T],
                                                                                   op0=ALU.mult, op1=ALU.add), r=[gsn, c1n, "FT"], w=[c1n])
                    S.op("dve", lambda e, fb=fb, gs=gs, c1=c1: e.scalar_tensor_tensor(out=c1[:, 0:T], in0=gs[:, 2:T + 2], scalar=FT[:, fb, 2:3], in1=c1[:, 0:T],
                                                                                   op0=ALU.mult, op1=ALU.add), r=[gsn, c1n, "FT"], w=[c1n])
                    S.op("act", lambda e, c1=c1, ge=ge: e.activation(out=ge[:, 0:T], in_=c1[:, 0:T], func=AF.Gelu_apprx_tanh), r=[c1n], w=[gen])
                    S.op("dve", lambda e, fb=fb, ge=ge, pv=pv: e.tensor_tensor(out=actf[:, fb * T:(fb + 1) * T], in0=pv[:], in1=ge[:, 0:T], op=ALU.mult),
                         r=[pvn, gen], w=["act%d" % fb] + (arena_names if first_tile else []))
                    if hook and fb >= 3:
                        hook.pop(0)()
            for half in range(2):
                phs = [bank() for _ in range(4)]
                for (kb0, nkb) in ((0, 8), (8, 8), (16, 6)):
                    sd, sdn = load_w(wb_dn, "wb_dn", kb0, nkb, T * half)
                    for j in range(4):
                        ph, phn = phs[j]
                        for kk in range(nkb):
                            fb = kb0 + kk
                            S.op("pe", lambda e, fb=fb, kk=kk, j=j, ph=ph, sd=sd: e.matmul(ph[:], lhsT=actf[:, fb * T + j * 128: fb * T + (j + 1) * 128], rhs=sd[:, kk, :],
                                                                                         start=(fb == 0), stop=(fb == NFB - 1)), r=[sdn, "act%d" % fb], w=[phn])
                for j in range(4):
                    ph, phn = phs[j]
                    S.op("dve", lambda e, j=j, half=half, ph=ph: e.tensor_tensor(out=xt[:, j, half * T:(half + 1) * T], in0=ph[:], in1=xt[:, j, half * T:(half + 1) * T], op=ALU.add),
                         r=[phn, xn], w=[xn])
            def final_steps():
                steps = []

                def sq(j, last):
                    def f():
                        S.op("act", lambda e: e.activation(out=junk[:], in_=xt[:, j, :], func=AF.Square, accum_out=ss[:, j:j + 1]),
                             r=[xn], w=["junk", "ss0"])
                        if last:
                            S.op("dve", lambda e: e.tensor_scalar(out=ms[:, 0:4], in0=ss[:, 0:4], scalar1=1.0 / D, scalar2=EPS, op0=ALU.mult, op1=ALU.add),
                                 r=["ss0"], w=["ms0"])
                            S.op("pool", lambda e: e.tensor_tensor(out=rstd[:, 0:4], in0=ms[:, 0:4], in1=mhalf[:, 0:4], op=ALU.pow),
                                 r=["ms0", "mhalf"], w=["rstd0"])
                    return f

                def sc(j, last):
                    def f():
                        extra = [] if st.get("fin_started") else (["bufB"] + ["cvf%d" % c_ for c_ in range(8)])
                        S.op("act", lambda e: e.activation(out=bufB[:, j * 1024:(j + 1) * 1024], in_=xt[:, j, :], func=AF.Copy, scale=rstd[:, j:j + 1]),
                             r=[xn, "rstd0"], w=["fin%d" % j] + extra)
                        S.op("pool", lambda e: e.tensor_tensor(out=bufB[:, j * 1024:(j + 1) * 1024], in0=bufB[:, j * 1024:(j + 1) * 1024], in1=fg_rep, op=ALU.mult),
                             r=["fin%d" % j, "Bh"], w=["fin%d" % j])
                        if last:
                            st["fin_started"] = True
                            dsto = out_d.ap()[q, s0:s0 + T, :].rearrange("(j p) d -> p j d", p=128)
                            S.dma("pool", "st_bufB", lambda e: e.dma_start(out=dsto, in_=bufB[:].rearrange("p (j d) -> p j d", j=4)),
                                  r=["fin%d" % jj for jj in range(4)], w=["out%d:%d" % (q, i)])
                    return f
                for j in range(4):
                    steps.append(sq(j, j == 3))
                for j in range(4):
                    steps.append(sc(j, j == 3))
                return steps
            return final_steps()

        if tiles is None:
            tiles = [(q, i) for q in range(NSEQ) for i in range(NT)]
            tl = [tiles, tiles, tiles]
        else:
            tl = [[(0, i) for i in range(min(NT, tiles + 2 - k))] for k in range(3)]
        if npass >= 1:
            run_pass1(tl[0])
        else:
            prep_some(1000)
            run_deferred(1000)
        if npass >= 2:
            run_pass2(tl[1])
        prep_some(1000)
        if npass >= 3:
            S.dma("sp", "setup1", lambda e: e.dma_start(out=fg_rep[:], in_=bass.AP(final_g, 0, [[0, 128], [1, D]])), w=["Bh"])
            st["nring"] = NW + 2
            st["nbank"] = 8
            pend3 = None
            for (q, i) in tl[2]:
                pend3 = pass3(q, i, pend3)
            while pend3:
                pend3.pop(0)()

        keys = ["pe", "act", "dve", "pool"] + ["dma:" + k for k in S.dma_keys()]
        sems = {k: es.enter_context(nc.semaphore(k.replace(":", "_"))) for k in keys}
        cnt = S.emit(sems)
        for k, v in cnt.items():
            if k.startswith("dma:"):
                nc.sync.wait_ge(sems[k], v)
        print("ops:", len(S.ops), "sem counts:", {k: v for k, v in cnt.items() if not k.startswith("dma:")})
    return nc


_NC_CACHE = {}


def kernel(x, norm1_g, w_in, sg_ln_g, sg_ln_b, sg_w, sg_b, w_a_out, cv_dw_w, cv_dw_b,
           cv_ln_g, cv_ln_b, w_b_out, w_o, norm2_g, w_up, ffn_dw_w, ffn_dw_b, w_down, final_g):
    n = 8
    f = lambda a: np.ascontiguousarray(np.asarray(a, dtype=np.float32))
    x = f(x)
    shared = dict(norm1_g=f(norm1_g), w_in=f(w_in), sg_ln_g=f(sg_ln_g), sg_ln_b=f(sg_ln_b), sg_w=f(sg_w), sg_b=f(sg_b),
                  w_a_out=f(w_a_out), cv_dw_w=f(cv_dw_w), cv_dw_b=f(cv_dw_b), cv_ln_g=f(cv_ln_g), cv_ln_b=f(cv_ln_b),
                  w_b_out=f(w_b_out), w_o=f(w_o), norm2_g=f(norm2_g), w_up=f(w_up), ffn_dw_w=f(ffn_dw_w),
                  ffn_dw_b=f(ffn_dw_b), w_down=f(w_down), final_g=f(final_g))
    if "nc" not in _NC_CACHE:
        _NC_CACHE["nc"] = build_nc()
    nc = _NC_CACHE["nc"]
    in_maps = [dict(shared, x=x[NSEQ * c:NSEQ * (c + 1)]) for c in range(n)]
    res = run_bass_kernel_spmd(nc, in_maps, core_ids=list(range(n)))
    return np.concatenate([np.asarray(r["out"]) for r in res.results], axis=0).astype(np.float32)
```
